# Optimizing a Trainium2 kernel written in Bass

```python
import jax, jax.numpy as jnp
from jax import lax
import numpy as np

D_MODEL = 1024
BATCH = 8
SEQ = 2048
DEPTH = 1

ATTN_HEADS = 8
HEAD_DIM = 64
D_ATTN = ATTN_HEADS * HEAD_DIM
MOBA_BLOCK = 256
MOBA_TOPK = 3
Q_CHUNK = 32
ROPE_THETA = 10000.0
POOL_WINDOWS = (2, 4, 8, 16)
N_POOL_GROUPS = len(POOL_WINDOWS)
POOL_GROUP_DIM = 128
D_POOL = N_POOL_GROUPS * POOL_GROUP_DIM
N_BRANCHES = 2
D_IN_PROJ = 3 * D_ATTN + D_POOL + N_BRANCHES * D_MODEL
D_FF = 2816
CONV_WIDTH = 3
LN_EPS = 1e-5
DEEPNORM_ALPHA = (2.0 * DEPTH) ** 0.25
DEEPNORM_BETA = (8.0 * DEPTH) ** -0.25
NEG = -1e30

kernel_name = "hybrid_moba_pool_convffn_deepnorm"


def layer_norm(x, g, b):
    xf = x.astype(jnp.float32)
    mu = jnp.mean(xf, axis=-1, keepdims=True)
    var = jnp.mean(jnp.square(xf - mu), axis=-1, keepdims=True)
    y = (xf - mu) * lax.rsqrt(var + LN_EPS) * g.astype(jnp.float32) + b.astype(jnp.float32)
    return y.astype(x.dtype)


def rope_tables(s):
    half = HEAD_DIM // 2
    inv_freq = 1.0 / (ROPE_THETA ** (jnp.arange(half, dtype=jnp.float32) / half))
    ang = jnp.arange(s, dtype=jnp.float32)[:, None] * inv_freq[None, :]
    return jnp.cos(ang), jnp.sin(ang)


def apply_rope(t, cos, sin):
    tf = t.astype(jnp.float32)
    half = HEAD_DIM // 2
    t1, t2 = tf[..., :half], tf[..., half:]
    return jnp.concatenate([t1 * cos - t2 * sin, t2 * cos + t1 * sin], axis=-1).astype(t.dtype)


def moba_attention(q, k, v):
    b, h, s, d = q.shape
    nb = -(-s // MOBA_BLOCK)
    s_pad = nb * MOBA_BLOCK
    pad = ((0, 0), (0, 0), (0, s_pad - s), (0, 0))
    q, k, v = jnp.pad(q, pad), jnp.pad(k, pad), jnp.pad(v, pad)
    k_blocks = k.reshape(b, h, nb, MOBA_BLOCK, d)
    v_blocks = v.reshape(b, h, nb, MOBA_BLOCK, d)
    k_mean = jnp.mean(k_blocks.astype(jnp.float32), axis=3)
    q_block = jnp.arange(s_pad) // MOBA_BLOCK
    gate = jnp.einsum('bhsd,bhnd->bhsn', q.astype(jnp.float32), k_mean)
    past = jnp.arange(nb)[None, :] < q_block[:, None]
    gate = jnp.where(past, gate, NEG)
    n_sel = min(MOBA_TOPK, nb)
    _, sel = lax.top_k(gate, n_sel)
    sel_valid = sel < q_block[:, None]

    nc = s_pad // Q_CHUNK

    def to_chunks(t):
        return jnp.moveaxis(t.reshape(b, h, nc, Q_CHUNK, *t.shape[3:]), 2, 0)

    gather = jax.vmap(jax.vmap(lambda blocks, idx: blocks[idx]))
    scale = HEAD_DIM ** -0.5
    key_off = jnp.arange(MOBA_BLOCK)

    def one_chunk(args):
        c, qc, sc, vc = args
        qpos = c * Q_CHUNK + jnp.arange(Q_CHUNK)
        own = (c * Q_CHUNK) // MOBA_BLOCK
        k_own = lax.dynamic_index_in_dim(k_blocks, own, axis=2, keepdims=False).astype(jnp.float32)
        v_own = lax.dynamic_index_in_dim(v_blocks, own, axis=2, keepdims=False).astype(jnp.float32)
        k_sel = gather(k_blocks, sc).astype(jnp.float32)
        v_sel = gather(v_blocks, sc).astype(jnp.float32)
        qf = qc.astype(jnp.float32) * scale
        s_sel = jnp.einsum('bhqd,bhqnkd->bhqnk', qf, k_sel)
        s_sel = jnp.where(vc[..., None], s_sel, NEG).reshape(b, h, Q_CHUNK, n_sel * MOBA_BLOCK)
        s_own = jnp.einsum('bhqd,bhkd->bhqk', qf, k_own)
        causal = (own * MOBA_BLOCK + key_off)[None, :] <= qpos[:, None]
        s_own = jnp.where(causal, s_own, NEG)
        p = jax.nn.softmax(jnp.concatenate([s_sel, s_own], axis=-1), axis=-1)
        p_sel = p[..., :n_sel * MOBA_BLOCK].reshape(b, h, Q_CHUNK, n_sel, MOBA_BLOCK)
        p_own = p[..., n_sel * MOBA_BLOCK:]
        o = (jnp.einsum('bhqnk,bhqnkd->bhqd', p_sel, v_sel)
             + jnp.einsum('bhqk,bhkd->bhqd', p_own, v_own))
        return o.astype(q.dtype)

    out = lax.map(one_chunk, (jnp.arange(nc), to_chunks(q), to_chunks(sel), to_chunks(sel_valid)))
    out = jnp.moveaxis(out, 0, 2).reshape(b, h, s_pad, d)
    return out[:, :, :s]


def multiscale_pool(u, w_pool, pool_scale):
    b, s, _ = u.shape
    uf = u.astype(jnp.float32)
    cs = jnp.pad(lax.cumsum(uf, axis=1), ((0, 0), (1, 0), (0, 0)))
    t = jnp.arange(s)
    outs = []
    for g, w in enumerate(POOL_WINDOWS):
        sl = slice(g * POOL_GROUP_DIM, (g + 1) * POOL_GROUP_DIM)
        start = jnp.maximum(t + 1 - w, 0)
        count = (t + 1 - start).astype(jnp.float32)
        win_sum = cs[:, 1:, sl] - cs[:, start, sl]
        outs.append(win_sum / count[None, :, None] - uf[:, :, sl])
    pooled = jnp.stack(outs, axis=2)
    mixed = jnp.einsum('bsgc,gcd->bsgd', pooled, w_pool.astype(jnp.float32)).reshape(b, s, D_POOL)
    return (mixed * pool_scale.astype(jnp.float32)).astype(u.dtype)


def conv_ffn(x, w_ffn_gate, w_ffn_up, conv_w, conv_b, w_ffn_down):
    s = x.shape[1]
    a = x @ w_ffn_gate
    u = x @ w_ffn_up
    ap = jnp.pad(a, ((0, 0), (CONV_WIDTH - 1, 0), (0, 0)))
    a = sum(ap[:, i:i + s] * conv_w[i] for i in range(CONV_WIDTH)) + conv_b
    return (jax.nn.gelu(a, approximate=False) * u) @ w_ffn_down


def setup_inputs(seed: int = 0) -> dict:
    key = jax.random.key(seed)
    ks = jax.random.split(key, 20)
    L, D = DEPTH, D_MODEL
    f32 = jnp.float32

    def nrm(k, shape, fan_in, gain=1.0):
        return (jax.random.normal(k, shape, f32) * (gain * fan_in ** -0.5)).astype(f32)

    return {
        "x": jax.random.normal(ks[0], (BATCH, SEQ, D), f32),
        "w_in": nrm(ks[1], (L, D, D_IN_PROJ), D),
        "b_gate": 0.02 * jax.random.normal(ks[2], (L, N_BRANCHES * D), f32),
        "w_branch_attn": nrm(ks[3], (L, D_ATTN, D), D_ATTN),
        "w_pool": nrm(ks[4], (L, N_POOL_GROUPS, POOL_GROUP_DIM, POOL_GROUP_DIM), POOL_GROUP_DIM),
        "pool_scale": 1.0 + 0.05 * jax.random.normal(ks[5], (L, D_POOL), f32),
        "w_branch_pool": nrm(ks[6], (L, D_POOL, D), D_POOL),
        "w_out": nrm(ks[7], (L, D, D), D, DEEPNORM_BETA),
        "ln1_g": 1.0 + 0.05 * jax.random.normal(ks[8], (L, D), f32),
        "ln1_b": 0.02 * jax.random.normal(ks[9], (L, D), f32),
        "w_ffn_gate": nrm(ks[10], (L, D, D_FF), D),
        "w_ffn_up": nrm(ks[11], (L, D, D_FF), D),
        "conv_w": nrm(ks[12], (L, CONV_WIDTH, D_FF), CONV_WIDTH),
        "conv_b": 0.02 * jax.random.normal(ks[13], (L, D_FF), f32),
        "w_ffn_down": nrm(ks[14], (L, D_FF, D), D_FF, DEEPNORM_BETA),
        "ln2_g": 1.0 + 0.05 * jax.random.normal(ks[15], (L, D), f32),
        "ln2_b": 0.02 * jax.random.normal(ks[16], (L, D), f32),
    }


def reference(x, w_in, b_gate, w_branch_attn, w_pool, pool_scale, w_branch_pool, w_out,
              ln1_g, ln1_b, w_ffn_gate, w_ffn_up, conv_w, conv_b, w_ffn_down, ln2_g, ln2_b):
    b, s, _ = x.shape
    cos, sin = rope_tables(s)
    for l in range(DEPTH):
        z = x @ w_in[l]
        o0, o1, o2, o3 = D_ATTN, 2 * D_ATTN, 3 * D_ATTN, 3 * D_ATTN + D_POOL
        heads = lambda t: t.reshape(b, s, ATTN_HEADS, HEAD_DIM).transpose(0, 2, 1, 3)
        q = apply_rope(heads(z[..., :o0]), cos, sin)
        k = apply_rope(heads(z[..., o0:o1]), cos, sin)
        v = heads(z[..., o1:o2])
        u_pool = z[..., o2:o3]
        gates = jax.nn.sigmoid(z[..., o3:] + b_gate[l])
        g_attn, g_pool = gates[..., :D_MODEL], gates[..., D_MODEL:]
        y_attn = moba_attention(q, k, v).transpose(0, 2, 1, 3).reshape(b, s, D_ATTN)
        y_attn = y_attn @ w_branch_attn[l]
        y_pool = multiscale_pool(u_pool, w_pool[l], pool_scale[l]) @ w_branch_pool[l]
        mix = (g_attn * y_attn + g_pool * y_pool) @ w_out[l]
        x = layer_norm(DEEPNORM_ALPHA * x + mix, ln1_g[l], ln1_b[l])
        ffn = conv_ffn(x, w_ffn_gate[l], w_ffn_up[l], conv_w[l], conv_b[l], w_ffn_down[l])
        x = layer_norm(DEEPNORM_ALPHA * x + ffn, ln2_g[l], ln2_b[l])
    return x
```

```python
import bisect
import numpy as np
import concourse.bass as bass
import concourse.mybir as mybir
from concourse.bass_utils import run_bass_kernel_spmd

F32 = mybir.dt.float32
BF16 = mybir.dt.bfloat16
U8 = mybir.dt.uint8
AF = mybir.ActivationFunctionType
ALU = mybir.AluOpType
AX = mybir.AxisListType

S_ = 2048
D = 1024
H = 8
NBLK = 8
DFF = 2816
NF = 22
ALPHA = float(2.0 ** 0.25)
EPS = 1e-5
BIGNEG = -262144.0
KB = 1024
SYNC_ALL_SAME_ENGINE = True
SAME_ENG_RAW_DIST = 3


class _Op:
    __slots__ = ("eng", "fn", "reads", "writes", "deps", "sig", "sem", "ticket", "inc",
                 "idx", "eidx", "is_dma", "waits", "clock")


class _IMap:
    def __init__(self, size):
        self.b = [0, size]
        self.s = [[None, {}]]

    def _split(self, x):
        i = bisect.bisect_left(self.b, x)
        if i < len(self.b) and self.b[i] == x:
            return
        st = self.s[i - 1]
        self.b.insert(i, x)
        self.s.insert(i, [st[0], dict(st[1])])

    def segs(self, lo, hi):
        self._split(lo)
        self._split(hi)
        i = bisect.bisect_left(self.b, lo)
        j = bisect.bisect_left(self.b, hi)
        return self.s[i:j]


class Sched:
    COMPUTE = ("pe", "act", "dve", "pool")

    def __init__(self, nc):
        self.nc = nc
        self.ops = []
        self.tens = {}
        self.maps = {}
        self.keys = {}

    def reg(self, handle, space, base, size):
        rowlen = 1
        for d in list(handle.shape)[1:]:
            rowlen *= int(d)
        esize = 2 if handle.dtype == BF16 else (1 if handle.dtype == U8 else 4)
        self.tens[handle.name] = (space, base, esize, rowlen)
        if space not in self.maps:
            self.maps[space] = _IMap(size)

    def _rng(self, x):
        if isinstance(x, (str, tuple)):
            return [("key", x, 0, 0)]
        space, base, esize, rowlen = self.tens[x.tensor.name]
        if space != "sb":
            return [("iv", space, 0, 2048)]
        dims = [(int(st), int(n)) for (st, n) in list(x.ap)[1:] if int(n) > 1]
        col = int(x.offset) % rowlen
        ivs = [(col, col + 1)]
        for (st, n) in sorted(dims, key=lambda d: abs(d[0])):
            if st < 0:
                ivs = [(a + st * (n - 1), b + st * (n - 1)) for (a, b) in ivs]
                st = -st
            span = ivs[-1][1] - ivs[0][0]
            if len(ivs) == 1 and st <= span:
                ivs = [(ivs[0][0], ivs[0][1] + st * (n - 1))]
            elif len(ivs) * n <= 64:
                ivs = [(a + i * st, b + i * st) for i in range(n) for (a, b) in ivs]
                ivs.sort()
                m = [ivs[0]]
                for (a, b) in ivs[1:]:
                    if a <= m[-1][1]:
                        m[-1] = (m[-1][0], max(m[-1][1], b))
                    else:
                        m.append((a, b))
                ivs = m
            else:
                ivs = [(ivs[0][0], ivs[-1][1] + st * (n - 1))]
        return [("iv", space, base + a * esize, base + b * esize) for (a, b) in ivs]

    def op(self, eng, fn, reads=(), writes=()):
        o = _Op()
        o.eng, o.fn, o.is_dma = eng, fn, False
        o.reads = [q for r in reads for q in self._rng(r)]
        o.writes = [q for w in writes for q in self._rng(w)]
        o.idx = len(self.ops)
        self.ops.append(o)
        return o

    def dma(self, queue, fn, reads=(), writes=()):
        o = self.op(queue, fn, reads, writes)
        o.is_dma = True
        return o

    def _states(self, r):
        if r[0] == "key":
            st = self.keys.get(r[1])
            if st is None:
                st = [None, {}]
                self.keys[r[1]] = st
            return [st]
        return self.maps[r[1]].segs(r[2], r[3])

    def _analyse(self):
        ecount = {}
        for o in self.ops:
            o.eidx = ecount.get(o.eng, 0)
            ecount[o.eng] = o.eidx + 1
            deps = {}
            rk = ("dma", o.idx) if o.is_dma else o.eng
            for r in o.reads:
                for st in self._states(r):
                    w = st[0]
                    if w is not None:
                        deps[w.idx] = (w, True)
            for wv in o.writes:
                for st in self._states(wv):
                    w = st[0]
                    if w is not None and w.idx not in deps:
                        deps[w.idx] = (w, False)
                    for rd in st[1].values():
                        if rd.idx not in deps:
                            deps[rd.idx] = (rd, False)
            for r in o.reads:
                for st in self._states(r):
                    st[1][rk] = o
            for wv in o.writes:
                for st in self._states(wv):
                    st[0] = o
                    st[1].clear()
            need = []
            for d, raw in deps.values():
                if d is o:
                    continue
                if d.is_dma or o.is_dma or d.eng != o.eng:
                    need.append(d)
                elif o.eng != "pe" and (SYNC_ALL_SAME_ENGINE or (raw and (o.eidx - d.eidx) < SAME_ENG_RAW_DIST)):
                    need.append(d)
            o.deps = need
            o.reads = o.writes = None
        for o in self.ops:
            o.sig = o.is_dma
        for o in self.ops:
            for d in o.deps:
                d.sig = True

    def emit(self, sems):
        self._analyse()
        nc = self.nc
        known = {}
        counters = {e: 0 for e in self.COMPUTE}
        dma_n = {}
        dma_last = {}
        for o in self.ops:
            kn = known.setdefault(o.eng, {})
            waits = []

            def need(semkey, val, clock):
                if kn.get(semkey, 0) >= val:
                    return
                waits.append((semkey, val))
                for k2, v2 in clock.items():
                    if kn.get(k2, 0) < v2:
                        kn[k2] = v2

            for d in sorted(o.deps, key=lambda d: d.idx):
                need(d.sem, d.ticket, d.clock)
            if o.is_dma:
                pool = sems["dma:" + o.eng]
                n = dma_n.get(o.eng, 0)
                dma_n[o.eng] = n + 1
                semkey = ("dma", o.eng, n % len(pool))
                prev = dma_last.get(semkey, 0)
                if prev:
                    need(semkey, prev, {semkey: prev})
                o.sem, o.ticket, o.inc = semkey, prev + 16, 16
                dma_last[semkey] = o.ticket
            elif o.sig:
                counters[o.eng] += 1
                o.sem, o.ticket, o.inc = o.eng, counters[o.eng], 1
            o.waits = waits
            if o.sig:
                c = dict(kn)
                c[o.sem] = o.ticket
                o.clock = c
            o.deps = None

        def semh(key):
            if isinstance(key, tuple):
                return sems["dma:" + key[1]][key[2]]
            return sems[key]

        per_eng = {}
        for o in self.ops:
            per_eng.setdefault(o.eng, []).append(o)

        def run(engname, e):
            for o in per_eng.get(engname, []):
                for (sk, val) in o.waits:
                    e.wait_ge(semh(sk), val)
                inst = o.fn(e)
                if o.sig:
                    inst.then_inc(semh(o.sem), o.inc)

        with nc.Block() as block:
            @block.tensor
            def _(e):
                run("pe", e)

            @block.scalar
            def _(e):
                run("act", e)

            @block.vector
            def _(e):
                run("dve", e)

            @block.gpsimd
            def _(e):
                run("pool", e)

            @block.sync
            def _(e):
                run("sp", e)
        return dict(n_ops=len(self.ops), n_waits=sum(len(o.waits) for o in self.ops),
                    counters=counters, per_eng={k: len(v) for k, v in per_eng.items()})


def build_program(stop_after=None, dbg=None):
    nc = bass.Bass("TRN2", target_bir_lowering=False)
    es_tensors = []

    def din(name, shape):
        return nc.dram_tensor(name, list(shape), F32, kind="ExternalInput").ap()

    xT_d = din("xT", [D, S_])
    x_d = din("x", [S_, D])
    wqk_d = din("wqk", [16, 128, 8, 128])
    wv_d = din("wv", [128, 8, 512])
    wu_d = din("wu", [4, 128, 8, 128])
    wg_d = din("wg", [16, 128, 8, 128])
    wba_d = din("wba", [128, 4, 1024])
    wbp_d = din("wbp", [128, 4, 1024])
    wpool_d = din("wpool", [128, 4, 128])
    wout_d = din("wout", [128, 8, 1024])
    wgu_d = din("wgu", [NF, 128, 2, 8, 128])
    wd_d = din("wd", [8, 128, NF, 128])
    lnp_d = din("lnp", [4, 128, 1024])
    ropeC_d = din("ropeC", [128, S_])
    ropeS_d = din("ropeS", [128, S_])
    ident_d = din("ident", [128, 128])
    mAB_d = din("mAB", [128, 512])
    oh_d = din("oh", [8, S_])
    am_d = din("am", [128, 8, 64])
    small_d = din("small", [128, 128])
    out_d = nc.dram_tensor("out", [S_, D], F32, kind="ExternalOutput").ap()
    dbg_d = None
    if dbg is not None:
        dbg_d = nc.dram_tensor("dbg", list(dbg), F32, kind="ExternalOutput").ap()

    ARENA = 206 * KB
    arena_cm = nc.sbuf_tensor("arena", [128, ARENA], U8)
    arena = arena_cm.__enter__()
    abase = int(nc.lookup_mloc(arena).addr)
    S = Sched(nc)

    def T(name, shape, dt, off):
        h = nc.alloc_sbuf_tensor_at(name, list(shape), dt, offset=abase + off)
        S.reg(h, "sb", abase + off, 256 * KB)
        return h

    o = 0
    identb = T("identb", [128, 128], BF16, o); o += 256
    identf = T("identf", [128, 128], F32, o); o += 512
    mAB = T("mABb", [128, 512], BF16, o); o += 1024
    AM = T("AM", [128, 8, 64], F32, o); o += 2048
    small = T("small", [128, 128], F32, o); o += 512
    halo = T("halo", [128, NF, 2], F32, o); o += 256
    stt_ = T("lnst", [128, 8, 12], F32, o); o += 512
    mv_ = T("lnmv", [128, 8, 4], F32, o); o += 128
    assert o <= 6 * KB
    RA = 6 * KB
    RB = RA + 32 * KB
    RC = RB + 33 * KB
    RX = RC + 33 * KB
    assert RX + 100 * KB <= ARENA
    EXTRA = RX + 90 * KB
    rcw = small[:, 0:16]
    bgate = small[:, 16:32]
    pscale = small[:, 32:36]
    convb = small[:, 36:58]
    convw = small[:, 58:124].rearrange("p (f k) -> p f k", k=3)

    xTb = T("xTb", [128, 8, S_], BF16, RA)
    x1 = T("x1", [128, 8, 1024], F32, RA)
    U = T("U", [128, 4, 2064], F32, RB)
    qh = [T(f"q{h}", [128, S_], BF16, RB + h * 4 * KB) for h in range(H)]
    mixinT = T("mixinT", [128, 8, S_], BF16, RB)
    hTa = T("hTa", [128, 16, 1024], BF16, RB)
    PA = T("PA", [128, 2064], F32, RC)
    PB = T("PB", [128, 2064], F32, RC + 8256)
    PL = T("PL", [128, 4, S_], BF16, RC + 16512)
    tmpf = T("tmpf", [128, 16], F32, RC + 16512 + 16384)
    kh = [T(f"k{h}", [128, S_], BF16, RC + h * 4 * KB) for h in range(H)]
    x1Tb = T("x1Tb", [128, 8, 1024], BF16, RC)
    wout = T("wout", [128, 8, 1024], BF16, RC + 16 * KB)
    hTb = T("hTb", [128, 6, 1024], BF16, RC + 16 * KB)
    poolinT = T("poolinT", [128, 4, S_], BF16, RX)
    attnT = T("attnT", [128, 4, S_], BF16, RX + 16 * KB)
    ropeC = T("ropeC", [128, S_], F32, RX + 16 * KB)
    ropeS = T("ropeS", [128, S_], F32, RX + 24 * KB)
    Vp = T("Vp", [128, 16, 8, 128], BF16, RX + 32 * KB)
    M0 = RX + 64 * KB
    wbp = T("wbp", [128, 4, 1024], BF16, M0)
    wba = T("wba", [128, 4, 1024], BF16, M0 + 15 * KB)
    wu = T("wu", [128, 4, 8, 128], BF16, M0)
    wpool = T("wpoolb", [128, 4, 128], BF16, M0 + 8 * KB)
    wqk = [T(f"wqk{i}", [128, 8, 128], BF16, M0 + i * 2 * KB) for i in range(3)]
    wv = T("wv", [128, 8, 512], BF16, M0 + 9 * KB)
    t1 = [T(f"t1_{i}", [128, 512], F32, M0 + 17 * KB + i * 2 * KB) for i in range(2)]
    t2 = [T(f"t2_{i}", [128, 512], F32, M0 + 21 * KB + i * 2 * KB) for i in range(2)]
    biasq = T("biasq", [128, 16, 128], BF16, M0)
    km32 = T("km32", [128, 64], F32, M0 + 25 * KB)
    kmb = T("kmb", [128, 64], BF16, M0 + 25 * KB + 256)
    junk = T("junk", [128, 256], BF16, M0 + 25 * KB + 512)
    tsel = [T(f"tsel{i}", [128, 64], F32, M0 + 5 * KB + 512 + i * 256) for i in range(2)]
    gm = [T(f"gm{i}", [128, 64], F32, M0 + 4 * KB + 512 + i * 256) for i in range(2)]
    srt = [T(f"srt{i}", [128, 64], F32, M0 + 5 * KB + i * 256) for i in range(2)]
    PT = [T(f"PT{i}", [128, 512], BF16, M0 + 8 * KB + i * KB) for i in range(5)]
    rl = [T(f"rl{i}", [128, 256], F32, M0 + 13 * KB + i * KB) for i in range(2)]
    wga = [T(f"wga{i}", [128, 8, 128], BF16, EXTRA + i * 2 * KB) for i in range(2)]
    wgp = [T(f"wgp{i}", [128, 8, 128], BF16, EXTRA + 4 * KB + i * 2 * KB) for i in range(2)]
    sga = [T(f"sga{i}", [128, 512], F32, RX + 32 * KB + i * 2 * KB) for i in range(2)]
    sgp = [T(f"sgp{i}", [128, 512], F32, RX + 32 * KB + 4 * KB + i * 2 * KB) for i in range(2)]
    m1 = [T(f"m1_{i}", [128, 512], F32, RX + 32 * KB + 8 * KB + i * 2 * KB) for i in range(2)]
    m2 = [T(f"m2_{i}", [128, 512], F32, RX + 32 * KB + 12 * KB + i * 2 * KB) for i in range(2)]
    x1H1 = T("x1H1", [128, 8, 1024], F32, RX)
    x1TH1 = T("x1TH1", [128, 8, 1024], BF16, RX + 32 * KB)
    WS = RX + 48 * KB

    pst = T("pst", [128, 8, 8, 6], F32, RX + 60 * KB)

    def hTf(f):
        return hTa[:, f, :] if f < 16 else hTb[:, f - 16, :]
    wgu = [T(f"wgu{i}", [128, 2, 8, 128], BF16, WS + i * 4 * KB) for i in range(2)]
    wd = [T(f"wd{i}", [128, NF, 128], BF16, WS + i * 5632) for i in range(2)]
    LNP = RX + 56 * KB
    ln1g = T("ln1g", [128, 1024], F32, LNP)
    ln1b = T("ln1b", [128, 1024], F32, LNP + 4 * KB)
    WK = RX + 64 * KB
    xres = [T(f"xres{i}", [128, 1024], F32, WK + i * 4 * KB) for i in range(3)]
    x1b = [T(f"x1b{i}", [128, 1024], BF16, WK + 12 * KB + i * 2 * KB) for i in range(2)]
    abuf = [T("abuf0", [128, 1032], F32, RC + 28 * KB), T("abuf1", [128, 1032], F32, RX + 89 * KB)]
    cbuf = [T(f"cbuf{i}", [128, 512], F32, RX + 94 * KB + i * 2 * KB) for i in range(2)]
    gbuf = [T("gbuf0", [128, 512], F32, RX + 98 * KB), T("gbuf1", [128, 512], F32, WK + 17 * KB)]
    oT = [T(f"oT{i}", [128, 512], F32, WK + 12 * KB + i * 2 * KB) for i in range(2)]
    ln2g = T("ln2g", [128, 1024], F32, WK)
    ln2b = T("ln2b", [128, 1024], F32, WK + 4 * KB)
    assert WK + 25 * KB <= RX + 90 * KB

    ps = []
    for i in range(6):
        p = nc.alloc_psum_tensor(f"ps{i}", [128, 512], F32)
        S.reg(p, f"ps{i}", 0, 4096)
        ps.append(p)
    psTs = []
    for i in range(2):
        p = nc.alloc_psum_tensor(f"psT{i}", [128, 1024], BF16)
        S.reg(p, f"psT{i}", 0, 4096)
        psTs.append(p)
    pst_rr = [0]

    def npsT():
        b = psTs[pst_rr[0] % 2]
        pst_rr[0] += 1
        return b

    sem_cms = []

    def mk_sem(name):
        cm = nc.semaphore(name)
        sem_cms.append(cm)
        return cm.__enter__()

    sems = {k: mk_sem("s_" + k) for k in ("pe", "act", "dve", "pool")}
    sems["dma:sp"] = [mk_sem(f"dsp{i}") for i in range(12)]
    sems["dma:pool"] = [mk_sem(f"dpl{i}") for i in range(16)]

    def ld_cast(dst, src):
        S.dma("pool", lambda e: e.dma_start(out=dst, in_=src), writes=[dst])

    def ld(dst, src):
        S.dma("sp", lambda e: e.dma_start(out=dst, in_=src), writes=[dst])

    def mm(out, lhsT, rhs, start, stop):
        S.op("pe", lambda e: e.matmul(out, lhsT, rhs, start=start, stop=stop),
             reads=[lhsT, rhs], writes=[out])

    def tr(out, in_, ident):
        S.op("pe", lambda e: e.transpose(out, in_, ident), reads=[in_, ident], writes=[out])

    def act(out, in_, func, bias=None, scale=None):
        rd = [in_]
        kw = {}
        if bias is not None:
            kw["bias"] = bias
            if not isinstance(bias, float):
                rd.append(bias)
        if scale is not None:
            kw["scale"] = scale
            if not isinstance(scale, float):
                rd.append(scale)
        S.op("act", lambda e: e.activation(out=out, in_=in_, func=func, **kw), reads=rd, writes=[out])

    def tt(eng, out, in0, in1, op):
        S.op(eng, lambda e: e.tensor_tensor(out=out, in0=in0, in1=in1, op=op), reads=[in0, in1], writes=[out])

    def ts(eng, out, in0, s1, s2, op0, op1=None):
        rd = [in0] + [s for s in (s1, s2) if s is not None and not isinstance(s, float)]
        if op1 is None:
            S.op(eng, lambda e: e.tensor_scalar(out=out, in0=in0, scalar1=s1, scalar2=s2, op0=op0),
                 reads=rd, writes=[out])
        else:
            S.op(eng, lambda e: e.tensor_scalar(out=out, in0=in0, scalar1=s1, scalar2=s2, op0=op0, op1=op1),
                 reads=rd, writes=[out])

    def stt(eng, out, in0, scalar, in1, op0, op1):
        rd = [in0, in1] + ([] if isinstance(scalar, float) else [scalar])
        S.op(eng, lambda e: e.scalar_tensor_tensor(out=out, in0=in0, scalar=scalar, in1=in1, op0=op0, op1=op1),
             reads=rd, writes=[out])

    def memset(eng, ap, val):
        S.op(eng, lambda e: e.memset(ap, val), writes=[ap])

    def cp(eng, out, in_):
        S.op(eng, lambda e: e.tensor_copy(out, in_), reads=[in_], writes=[out])

    def dump(ap_sb, rows, cols, r0=0, c0=0):
        S.dma("sp", lambda e: e.dma_start(out=dbg_d[r0:r0 + rows, c0:c0 + cols], in_=ap_sb),
              reads=[ap_sb], writes=["dbg"])

    def finish():
        S.op("sp", lambda e: e.nop(), reads=["out", "dbg"])
        info = S.emit(sems)
        for cm in reversed(sem_cms):
            cm.__exit__(None, None, None)
        arena_cm.__exit__(None, None, None)
        return nc, info

    ld(identf[:], ident_d)
    ld(AM[:], am_d)
    ld(small[:], small_d)
    xT_v = xT_d.rearrange("(dt p) t -> p dt t", p=128)
    ld_cast(xTb[:, :, 0:512], xT_v[:, :, 0:512])
    for g in range(4):
        ld_cast(wu[:, g, :, :], wu_d[g])
    for c in range(1, 4):
        ld_cast(xTb[:, :, c * 512:(c + 1) * 512], xT_v[:, :, c * 512:(c + 1) * 512])
    ld_cast(identb[:], ident_d)
    ld_cast(mAB[:], mAB_d)
    MA = mAB[:, 0:256]
    MB = mAB[:, 256:512]

    bank_rr = [0]

    def nbank():
        b = ps[bank_rr[0] % 6]
        bank_rr[0] += 1
        return b

    ld_cast(wpool[:], wpool_d)
    memset("dve", U[:, :, 0:16], 0.0)
    memset("dve", PA[:, 0:16], 0.0)
    memset("dve", PB[:, 0:16], 0.0)
    for c in range(4):
        for g in range(4):
            bk = nbank()
            for dt in range(8):
                mm(bk[:, :], wu[:, g, dt, :], xTb[:, dt, c * 512:(c + 1) * 512], dt == 0, dt == 7)
            act(U[:, g, 16 + c * 512:16 + (c + 1) * 512], bk[:, :], AF.Copy)
    ld_cast(wv[:], wv_d)
    memset("pool", Vp[:].rearrange("p t h c -> p (t h) c")[:, :, 64:128], 1.0)
    for t_ in range(16):
        bk = nbank()
        for dt in range(8):
            mm(bk[:, :], xTb[:, dt, t_ * 128:(t_ + 1) * 128], wv[:, dt, :], dt == 0, dt == 7)
        act(Vp[:, t_, :, 0:64], bk[:, :].rearrange("p (h c) -> p h c", c=64), AF.Copy)
    for w in range(3):
        ld_cast(wqk[w][:], wqk_d[w])
    for g in range(4):
        src = U[:, g, :]
        cur = src
        bufs = [PA, PB]
        for lev in range(g + 1):
            dst = bufs[lev % 2]
            sh = 2 ** lev
            tt("dve", dst[:, 16:2064], cur[:, 16:2064], cur[:, 16 - sh:2064 - sh], ALU.add)
            cur = dst[:, :]
        w = 2 ** (g + 1)
        stt("dve", PL[:, g, :], cur[:, 16:2064], 1.0 / w, src[:, 16:2064], ALU.mult, ALU.subtract)
        tt("dve", tmpf[:, 0:w - 1], cur[:, 16:16 + w - 1], rcw[:, 0:w - 1], ALU.mult)
        tt("dve", PL[:, g, 0:w - 1], tmpf[:, 0:w - 1], src[:, 16:16 + w - 1], ALU.subtract)
    for g in range(4):
        for c in range(4):
            bk = nbank()
            mm(bk[:, :], wpool[:, g, :], PL[:, g, c * 512:(c + 1) * 512], True, True)
            act(poolinT[:, g, c * 512:(c + 1) * 512], bk[:, :], AF.Copy, scale=pscale[:, g:g + 1])
    if stop_after == "pool":
        for g in range(4):
            cp("dve", t1[0][:, 0:512], poolinT[:, g, 0:512])
            dump(t1[0][:, 0:512], 128, 512, r0=g * 128)
        return finish()

    ld(ropeC[:], ropeC_d)
    ld(ropeS[:], ropeS_d)
    for h in range(H):
        ld_cast(kh[h][64:72, :], oh_d)
    pend = []
    ui = 0
    def kmean_head(h):
        for n in range(NBLK):
            S.op("act", (lambda h, n: lambda e: e.activation(
                out=junk[0:64, :], in_=kh[h][0:64, n * 256:(n + 1) * 256], func=AF.Copy,
                accum_out=km32[0:64, h * 8 + n:h * 8 + n + 1]))(h, n),
                reads=[kh[h][0:64, n * 256:(n + 1) * 256]],
                writes=[junk[0:64, :], km32[0:64, h * 8 + n:h * 8 + n + 1]])

    for w in range(16):
        if w - 1 >= 8:
            pass
        dstT = qh[w] if w < 8 else kh[w - 8]
        for c in range(4):
            bk = nbank()
            for dt in range(8):
                mm(bk[:, :], wqk[w % 3][:, dt, :], xTb[:, dt, c * 512:(c + 1) * 512], dt == 0, dt == 7)
            s = ui % 2
            ui += 1
            cs = slice(c * 512, (c + 1) * 512)
            tt("dve", t1[s][0:64, :], bk[0:64, :], ropeC[0:64, cs], ALU.mult)
            tt("dve", t2[s][0:64, :], bk[64:128, :], ropeS[64:128, cs], ALU.mult)
            if pend:
                a, b_, d_ = pend.pop()
                tt("pool", d_, a, b_, ALU.add)
            pend.append((t1[s][0:64, :], t2[s][0:64, :], dstT[0:64, cs]))
        if w + 3 < 16:
            ld_cast(wqk[w % 3][:], wqk_d[w + 3])
        if w >= 9:
            kmean_head(w - 9)
    a, b_, d_ = pend.pop()
    tt("pool", d_, a, b_, ALU.add)
    kmean_head(7)
    if stop_after == "qkv":
        cp("dve", t1[0][0:64, :], qh[0][0:64, 0:512])
        dump(t1[0][0:64, :], 64, 512, r0=0)
        cp("dve", t1[1][0:72, :], kh[1][0:72, 512:1024])
        dump(t1[1][0:72, :], 72, 512, r0=64)
        cp("dve", t2[0][:, 0:512].rearrange("p (h c) -> p h c", c=64), Vp[:, 3, :, 0:64])
        dump(t2[0][:, 0:512], 128, 512, r0=136)
        return finish()

    ts("dve", kmb[0:64, :], km32[0:64, :], 1.0 / 256.0, None, ALU.mult)
    if stop_after == "g1":
        cp("dve", t1[0][0:64, 0:64], km32[0:64, :])
        dump(t1[0][0:64, 0:64], 64, 64)
        return finish()
    memset("dve", biasq[:, :, :], 0.0)

    def gate_chunk(c):
        for qt in range(4 * c, 4 * c + 4):
            G = npsT()[:, 0:128].bitcast(F32)
            for h in range(H):
                mm(G[:, h * 8:(h + 1) * 8], qh[h][0:64, qt * 128:(qt + 1) * 128], kmb[0:64, h * 8:(h + 1) * 8], True, True)
            s = qt % 2
            tt("dve", gm[s][:, :], G, AM[:, qt // 2, :], ALU.add)
            for h in range(H):
                S.op("dve", (lambda s, h: lambda e: e.max(out=srt[s][:, h * 8:(h + 1) * 8], in_=gm[s][:, h * 8:(h + 1) * 8]))(s, h),
                     reads=[gm[s][:, h * 8:(h + 1) * 8]], writes=[srt[s][:, h * 8:(h + 1) * 8]])
            b3 = srt[s][:, 3:4]
            thr = bass.AP(b3.tensor, b3.offset, [list(b3.ap[0]), [8, 8], [0, 8]])
            S.op("dve", (lambda s, thr: lambda e: e.tensor_tensor(
                out=tsel[s][:, :].rearrange("p (h n) -> p h n", n=8), in0=gm[s][:, :].rearrange("p (h n) -> p h n", n=8),
                in1=thr, op=ALU.is_lt))(s, thr), reads=[gm[s][:, :], srt[s][:, :]], writes=[tsel[s][:, :]])
            ts("dve", biasq[:, qt, 64:128], tsel[s][:, :], BIGNEG, None, ALU.mult)

    if stop_after in ("g2", "gate"):
        for c in range(4):
            gate_chunk(c)
    if stop_after == "g2":
        cp("dve", t1[0][:, 0:128], biasq[:, 9, :])
        dump(t1[0][:, 0:128], 128, 128)
        return finish()
    def bias_rows(h, cs=range(4)):
        for c in cs:
            pT = npsT()
            for i in range(4):
                qt = c * 4 + i
                tr(pT[0:72, i * 128:(i + 1) * 128], biasq[:, qt, h * 8:h * 8 + 72], identb[:])
            act(qh[h][64:72, c * 512:(c + 1) * 512], pT[64:72, 0:512], AF.Copy)

    if stop_after == "gate":
        for h in range(H):
            bias_rows(h)
    if stop_after == "gate":
        cp("dve", t1[0][0:64, :], qh[2][0:64, 1024:1536])
        cp("dve", t1[0][64:72, :], qh[2][64:72, 1024:1536])
        dump(t1[0][0:72, :], 72, 512, r0=0)
        cp("dve", t1[1][0:64, :], qh[5][0:64, 1536:2048])
        cp("dve", t1[1][64:72, :], qh[5][64:72, 1536:2048])
        dump(t1[1][0:72, :], 72, 512, r0=72)
        return finish()

    ld_cast(wba[:], wba_d)
    for j in range(2):
        ld_cast(wga[j][:], wg_d[j])
        ld_cast(wgp[j][:], wg_d[8 + j])
    items = []
    for h in range(H):
        for b in range(NBLK):
            for p in range(b + 1):
                items.append((h, b, p))
    SB = [ps[0], ps[1], ps[2], ps[3]]
    ACC = [ps[4], ps[5]]
    acc_i = -1

    def vl(h, kt):
        return Vp[:, kt, h, :]

    def pv(i):
        h, b, p = items[i]
        ai = acc_of[i]
        acc = ACC[ai % 2][:, 0:256]
        pt = PT[i % 5]
        mm(acc, vl(h, 2 * p), pt[:, 0:256], p == 0, False)
        mm(acc, vl(h, 2 * p + 1), pt[:, 256:512], False, p == b)
        if p == b:
            r = rl[ai % 2]
            S.op("dve", lambda e: e.reciprocal(out=r[0:64, :], in_=acc[64:128, :]),
                 reads=[acc[64:128, :]], writes=[r[0:64, :]])
            po = (h % 2) * 64
            tt("dve", attnT[po:po + 64, h // 2, b * 256:(b + 1) * 256], acc[0:64, :], r[0:64, :], ALU.mult)

    acc_of = []
    for i, (h, b, p) in enumerate(items):
        if p == 0:
            acc_i += 1
        acc_of.append(acc_i)
    def gate_step(c):
        gate_chunk(c)
        bias_rows(0, [c])
        bias_rows(1, [c])

    gate_step(0)
    gate_step(1)
    for i, (h, b, p) in enumerate(items):
        if h == 0 and p == 0 and b in (0, 2):
            gate_step(b // 2 + 2)
        if b == 2 and p == 0 and h + 2 < H:
            bias_rows(h + 2)
            if h + 2 == H - 1:
                ld_cast(wbp[:], wbp_d)
        sb = SB[i % 4]
        qs = qh[h][0:72, b * 256:(b + 1) * 256]
        for half in range(2):
            kt = 2 * p + half
            o_ = sb[:, half * 256:(half + 1) * 256]
            mm(o_, kh[h][0:72, kt * 128:(kt + 1) * 128], qs, True, p != b)
            if p == b:
                mm(o_, identb[:], MA if half == 0 else MB, False, True)
        act(PT[i % 5][:, :], sb[:, :], AF.Exp, scale=0.125)
        if i >= 3:
            pv(i - 3)
    pv(len(items) - 3)
    pv(len(items) - 2)
    pv(len(items) - 1)
    if stop_after == "attn":
        for ct in range(4):
            cp("dve", t1[0][:, 0:512], attnT[:, ct, 1536:2048])
            dump(t1[0][:, 0:512], 128, 512, r0=ct * 128)
        return finish()

    ld_cast(wout[:], wout_d)
    ld(ln1g[:], lnp_d[0])
    ld(ln1b[:], lnp_d[1])
    ui = 0
    for j in range(8):
        js = slice(j * 128, (j + 1) * 128)
        for tc in range(4):
            cs = slice(tc * 512, (tc + 1) * 512)
            b0, b1, b2, b3 = nbank(), nbank(), nbank(), nbank()
            for dt in range(8):
                mm(b2[:, :], wga[j % 2][:, dt, :], xTb[:, dt, cs], dt == 0, dt == 7)
            for dt in range(8):
                mm(b3[:, :], wgp[j % 2][:, dt, :], xTb[:, dt, cs], dt == 0, dt == 7)
            for ct in range(4):
                mm(b0[:, :], wba[:, ct, js], attnT[:, ct, cs], ct == 0, ct == 3)
            for g in range(4):
                mm(b1[:, :], wbp[:, g, js], poolinT[:, g, cs], g == 0, g == 3)
            s = ui % 2
            ui += 1
            act(sga[s][:, :], b2[:, :], AF.Sigmoid, bias=bgate[:, j:j + 1])
            act(sgp[s][:, :], b3[:, :], AF.Sigmoid, bias=bgate[:, 8 + j:9 + j])
            tt("dve", m1[s][:, :], b0[:, :], sga[s][:, :], ALU.mult)
            tt("dve", m2[s][:, :], b1[:, :], sgp[s][:, :], ALU.mult)
            tt("pool", mixinT[:, j, cs], m1[s][:, :], m2[s][:, :], ALU.add)
        if j + 2 < 8:
            ld_cast(wga[j % 2][:], wg_d[j + 2])
            ld_cast(wgp[j % 2][:], wg_d[8 + j + 2])
    if stop_after == "merge":
        for j in range(8):
            cp("dve", t1[0][:, 0:512], mixinT[:, j, 512:1024])
            dump(t1[0][:, 0:512], 128, 512, r0=j * 128)
        return finish()

    def ln_A(row, s):
        st = stt_[:, s, :]
        mv = mv_[:, s, :]
        S.op("dve", lambda e: e.bn_stats(out=st[:, 0:6], in_=row[:, 0:512]), reads=[row[:, 0:512]], writes=[st[:, 0:6]])
        S.op("dve", lambda e: e.bn_stats(out=st[:, 6:12], in_=row[:, 512:1024]), reads=[row[:, 512:1024]], writes=[st[:, 6:12]])
        S.op("dve", lambda e: e.bn_aggr(out=mv[:, 0:2], in_=st), reads=[st], writes=[mv[:, 0:2]])
        ts("dve", mv[:, 1:2], mv[:, 1:2], EPS, None, ALU.add)
        act(mv[:, 2:3], mv[:, 1:2], AF.Sqrt)

    def ln_B(row, s):
        mv = mv_[:, s, :]
        S.op("dve", lambda e: e.reciprocal(out=mv[:, 2:3], in_=mv[:, 2:3]), reads=[mv[:, 2:3]], writes=[mv[:, 2:3]])
        ts("dve", mv[:, 3:4], mv[:, 0:1], mv[:, 2:3], -1.0, ALU.mult, ALU.mult)
        act(row, row, AF.Identity, bias=mv[:, 3:4], scale=mv[:, 2:3])

    def ln_C(row, g_, b_):
        tt("dve", row, row, g_[:, :], ALU.mult)
        tt("pool", row, row, b_[:, :], ALU.add)

    x1_of = [x1, x1H1]
    x1T_of = [x1Tb, x1TH1]

    def p6_mm(T_):
        xr = xres[T_ % 3]
        ld(xr[:], x_d[T_ * 128:(T_ + 1) * 128, :])
        bks = []
        for hv in range(2):
            bk = nbank()
            for j in range(8):
                mm(bk[:, :], mixinT[:, j, T_ * 128:(T_ + 1) * 128], wout[:, j, hv * 512:(hv + 1) * 512], j == 0, j == 7)
            bks.append(bk)
        return bks

    def p6_R(T_, bks):
        xr = xres[T_ % 3]
        row = x1_of[T_ // 8][:, T_ % 8, :]
        for hv in range(2):
            stt("dve", row[:, hv * 512:(hv + 1) * 512], xr[:, hv * 512:(hv + 1) * 512], ALPHA, bks[hv][:, :], ALU.mult, ALU.add)
        ln_A(row, T_ % 4)

    def p6_B(T_):
        ln_B(x1_of[T_ // 8][:, T_ % 8, :], T_ % 4)

    def p6_C(T_):
        row = x1_of[T_ // 8][:, T_ % 8, :]
        ln_C(row, ln1g, ln1b)
        act(x1b[T_ % 2][:, :], row, AF.Copy)

    def p6_tr(T_):
        xb_ = x1b[T_ % 2]
        tl = T_ % 8
        dstT = x1T_of[T_ // 8]
        pT = npsT()
        for dt in range(8):
            tr(pT[:, dt * 128:(dt + 1) * 128], xb_[:, dt * 128:(dt + 1) * 128], identb[:])
        if T_ % 2 == 0:
            act(dstT[:, :, tl * 128:(tl + 1) * 128], pT[:, :].rearrange("p (a b) -> p a b", b=128), AF.Copy)
        else:
            cp("dve", dstT[:, :, tl * 128:(tl + 1) * 128], pT[:, :].rearrange("p (a b) -> p a b", b=128))

    def ln_A0(row, s):
        st = stt_[:, s, :]
        mv = mv_[:, s, :]
        S.op("dve", lambda e: e.bn_stats(out=st[:, 0:6], in_=row[:, 0:512]), reads=[row[:, 0:512]], writes=[st[:, 0:6]])
        S.op("dve", lambda e: e.bn_stats(out=st[:, 6:12], in_=row[:, 512:1024]), reads=[row[:, 512:1024]], writes=[st[:, 6:12]])
        S.op("dve", lambda e: e.bn_aggr(out=mv[:, 0:2], in_=st), reads=[st], writes=[mv[:, 0:2]])
        ts("dve", mv[:, 1:2], mv[:, 1:2], EPS, None, ALU.add)

    def ln2_steps(x1h, half, batched, tiles=range(8)):
        def store(tl):
            t_ = half * 8 + tl
            ln_C(x1h[:, tl, :], ln2g, ln2b)
            row = x1h[:, tl, :]
            S.dma("sp", (lambda t_, row: lambda e: e.dma_start(out=out_d[t_ * 128:(t_ + 1) * 128, :], in_=row))(t_, row),
                  reads=[row], writes=["out"])
        if batched:
            for tl in range(8):
                yield (lambda tl=tl: ln_A0(x1h[:, tl, :], tl))
            yield (lambda: act(mv_[:, :, 2:3], mv_[:, :, 1:2], AF.Sqrt))
            for step in range(8 + 1):
                def emit(step=step):
                    if step < 8:
                        ln_B(x1h[:, step, :], step)
                    if 0 <= step - 1 < 8:
                        store(step - 1)
                yield emit
        else:
            tl_ = list(tiles)
            n_ = len(tl_)
            for step in range(n_ + 2):
                def emit(step=step):
                    if step < n_:
                        ln_A(x1h[:, tl_[step], :], tl_[step] % 4)
                    if 0 <= step - 1 < n_:
                        ln_B(x1h[:, tl_[step - 1], :], tl_[step - 1] % 4)
                    if 0 <= step - 2 < n_:
                        store(tl_[step - 2])
                yield emit

    def p7_preload():
        for f in range(2):
            ld_cast(wgu[f][:], wgu_d[f])

    def p7(half, xT, side=None):
        for _ in p7_gen(half, xT, side):
            pass

    def p7_gen(half, xT, side=None):
        ui = 0
        pend_tail = []

        def flush_tail():
            while pend_tail:
                s_, f_, cs_, bu_ = pend_tail.pop(0)
                act(gbuf[s_][:, :], cbuf[s_][:, :], AF.Gelu)
                tt("dve", hTf(f_)[:, cs_], gbuf[s_][:, :], bu_[:, :], ALU.mult)

        for f in range(NF):
            ab = abuf[f % 2]
            if half == 0:
                memset("dve", ab[:, 0:2], 0.0)
            else:
                cp("dve", ab[:, 0:2], halo[:, f, :])
            for c in range(2):
                cs = slice(c * 512, (c + 1) * 512)
                ba, bu = nbank(), nbank()
                for dt in range(8):
                    mm(ba[:, :], wgu[f % 2][:, 0, dt, :], xT[:, dt, cs], dt == 0, dt == 7)
                for dt in range(8):
                    mm(bu[:, :], wgu[f % 2][:, 1, dt, :], xT[:, dt, cs], dt == 0, dt == 7)
                s = ui % 2
                ui += 1
                act(ab[:, 2 + c * 512:2 + (c + 1) * 512], ba[:, :], AF.Copy)
                act(cbuf[s][:, :], ba[:, :], AF.Identity, bias=convb[:, f:f + 1], scale=convw[:, f, 2:3])
                stt("dve", cbuf[s][:, :], ab[:, 1 + c * 512:1 + (c + 1) * 512], convw[:, f, 1:2], cbuf[s][:, :], ALU.mult, ALU.add)
                stt("dve", cbuf[s][:, :], ab[:, c * 512:(c + 1) * 512], convw[:, f, 0:1], cbuf[s][:, :], ALU.mult, ALU.add)
                flush_tail()
                pend_tail.append((s, f, cs, bu))
            if half == 0:
                cp("dve", halo[:, f, :], ab[:, 1024:1026])
            if f + 2 < NF:
                ld_cast(wgu[f % 2][:], wgu_d[f + 2])
            if side is not None and f >= 2:
                nxt = next(side, None)
                if nxt is not None:
                    nxt()
            if f < NF - 1:
                yield f
        flush_tail()
        if side is not None:
            for nxt in side:
                nxt()
        yield NF - 1

    def p8(half, x1h, partial_stats=False):
        for j in range(2):
            ld_cast(wd[j][:], wd_d[j])

        def p8_mm(j):
            O = [nbank(), nbank()]
            for f in range(NF):
                for c in range(2):
                    mm(O[c][:, :], wd[j % 2][:, f, :], hTf(f)[:, c * 512:(c + 1) * 512], f == 0, f == NF - 1)
            return O

        ui = 0
        Onext = p8_mm(0)
        for j in range(8):
            O = Onext
            for c in range(2):
                act(oT[(ui + c) % 2][:, :], O[c][:, :], AF.Copy)
            if j + 1 < 8:
                Onext = p8_mm(j + 1)
            for c in range(2):
                s = ui % 2
                ui += 1
                tp = nbank()
                for i in range(4):
                    tr(tp[:, i * 128:(i + 1) * 128], oT[s][:, i * 128:(i + 1) * 128], identf[:])
                xv = x1h[:, c * 4:(c + 1) * 4, j * 128:(j + 1) * 128]
                stt("dve", xv, xv, ALPHA, tp[:, :].rearrange("p (a b) -> p a b", b=128), ALU.mult, ALU.add)
                if partial_stats:
                    for i in range(4):
                        tl = c * 4 + i
                        S.op("dve", (lambda tl, j: lambda e: e.bn_stats(out=pst[:, tl, j, :], in_=x1h[:, tl, j * 128:(j + 1) * 128]))(tl, j),
                             reads=[x1h[:, tl, j * 128:(j + 1) * 128]], writes=[pst[:, tl, j, :]])
            if j + 2 < 8:
                ld_cast(wd[j % 2][:], wd_d[j + 2])

    def ln2_tail(x1h, half):
        for tl in range(8):
            mv = mv_[:, tl, :]
            S.op("dve", (lambda tl, mv: lambda e: e.bn_aggr(out=mv[:, 0:2], in_=pst[:, tl, :, :].rearrange("p j k -> p (j k)")))(tl, mv),
                 reads=[pst[:, tl, :, :]], writes=[mv[:, 0:2]])
            ts("dve", mv[:, 1:2], mv[:, 1:2], EPS, None, ALU.add)
        act(mv_[:, :, 2:3], mv_[:, :, 1:2], AF.Sqrt)
        S.op("dve", lambda e: e.reciprocal(out=mv_[:, :, 2:3], in_=mv_[:, :, 2:3]), reads=[mv_[:, :, 2:3]], writes=[mv_[:, :, 2:3]])
        for tl in range(8):
            row = x1h[:, tl, :]
            mv = mv_[:, tl, :]
            t_ = half * 8 + tl
            stt("dve", row, row, mv[:, 0:1], ln2g[:, :], ALU.subtract, ALU.mult)
            stt("dve", row, row, mv[:, 2:3], ln2b[:, :], ALU.mult, ALU.add)
            S.dma("sp", (lambda t_, row: lambda e: e.dma_start(out=out_d[t_ * 128:(t_ + 1) * 128, :], in_=row))(t_, row),
                  reads=[row], writes=["out"])

    def p8_last(half, x1h):
        seq = [(c, j) for c in range(2) for j in range(8)]
        for k in range(2):
            ld_cast(wd[k][:], wd_d[seq[k][1]])

        def mm_(k):
            c, j = seq[k]
            O = nbank()
            for f in range(NF):
                mm(O[:, :], wd[k % 2][:, f, :], hTf(f)[:, c * 512:(c + 1) * 512], f == 0, f == NF - 1)
            return O

        side = None
        Onext = mm_(0)
        for k, (c, j) in enumerate(seq):
            O = Onext
            s = k % 2
            act(oT[s][:, :], O[:, :], AF.Copy)
            if k + 1 < len(seq):
                Onext = mm_(k + 1)
            tp = nbank()
            for i in range(4):
                tr(tp[:, i * 128:(i + 1) * 128], oT[s][:, i * 128:(i + 1) * 128], identf[:])
            xv = x1h[:, c * 4:(c + 1) * 4, j * 128:(j + 1) * 128]
            stt("dve", xv, xv, ALPHA, tp[:, :].rearrange("p (a b) -> p a b", b=128), ALU.mult, ALU.add)
            if k + 2 < len(seq):
                ld_cast(wd[k % 2][:], wd_d[seq[k + 2][1]])
            if c == 1 and j == 0:
                side = ln2_steps(x1h, half, False, tiles=range(0, 4))
            if side is not None:
                nxt = next(side, None)
                if nxt is not None:
                    nxt()
        for nxt in side:
            nxt()
        for emit in ln2_steps(x1h, half, False, tiles=range(4, 8)):
            emit()

    order = list(range(16))
    NT = 16
    pre = {0: p6_mm(order[0]), 1: p6_mm(order[1])}

    p7_preload()
    g7 = p7_gen(0, x1Tb)
    for step in range(NT + 3):
        if step + 2 < NT:
            pre[step + 2] = p6_mm(order[step + 2])
        if step < NT:
            p6_R(order[step], pre.pop(step))
        if 0 <= step - 1 < NT:
            p6_B(order[step - 1])
        if 0 <= step - 2 < NT:
            p6_C(order[step - 2])
        if 0 <= step - 3 < NT:
            p6_tr(order[step - 3])
        if step >= NT - 2:
            next(g7, None)
    if stop_after == "ln1":
        for tl in range(4):
            dump(x1[:, tl, :], 128, 1024, r0=tl * 128)
        for dt in range(2):
            cp("dve", xres[0][:, :], x1Tb[:, dt, :])
            dump(xres[0][:, :], 128, 1024, r0=512 + dt * 128)
        return finish()
    ld(ln2g[:], lnp_d[2])
    ld(ln2b[:], lnp_d[3])
    for _ in g7:
        pass
    if stop_after == "ffn_h":
        for f in range(4):
            cp("dve", xres[0][:, :], hTf(f * 5))
            dump(xres[0][:, :], 128, 1024, r0=f * 128)
        return finish()
    p8(0, x1)
    p7_preload()
    p7(1, x1TH1, side=ln2_steps(x1, 0, True))
    act(small[:, 126:127], small[:, 124:125], AF.Sqrt)
    p8(1, x1H1, partial_stats=True)
    ln2_tail(x1H1, 1)
    return finish()


def _rope_tables():
    half = 32
    inv = (1.0 / (np.float32(10000.0) ** (np.arange(half, dtype=np.float32) / np.float32(half)))).astype(np.float32)
    ang = (np.arange(S_, dtype=np.float32)[:, None] * inv[None, :]).astype(np.float32)
    cos = np.cos(ang).astype(np.float32).T
    sin = np.sin(ang).astype(np.float32).T
    c64 = np.concatenate([cos, cos], 0)
    s64 = np.concatenate([-sin, sin], 0)
    return (np.ascontiguousarray(np.concatenate([c64, c64], 0)),
            np.ascontiguousarray(np.concatenate([s64, s64], 0)))


def _consts():
    ropeC, ropeS = _rope_tables()
    ident = np.eye(128, dtype=np.float32)
    k = np.arange(128)[:, None]
    q = np.arange(256)[None, :]
    MA = np.where(k <= q, 0.0, BIGNEG).astype(np.float32)
    MB = np.where(k + 128 <= q, 0.0, BIGNEG).astype(np.float32)
    mAB = np.ascontiguousarray(np.concatenate([MA, MB], 1))
    oh = (np.arange(S_)[None, :] // 256 == np.arange(8)[:, None]).astype(np.float32)
    am = np.zeros((8, 8), np.float32)
    for b in range(8):
        for n in range(8):
            am[b, n] = 0.0 if n < b else (1e30 if n == b else -1e30)
    am = np.ascontiguousarray(np.broadcast_to(np.tile(am[:, None, :], (1, 8, 1)).reshape(1, 8, 64), (128, 8, 64)))
    return dict(ropeC=ropeC, ropeS=ropeS, ident=ident, mAB=mAB, oh=np.ascontiguousarray(oh), am=am)


def _ptile(w, nk):
    return np.ascontiguousarray(w.reshape(nk, 128, -1).transpose(1, 0, 2))


def prep_shared(inp):
    f = lambda k: np.asarray(inp[k], dtype=np.float32)
    w_in = f("w_in")[0]
    sh = dict(_consts())
    perm = (np.arange(64) + 32) % 64
    tiles = []
    for base in (0, 512):
        for h in range(8):
            cols = np.concatenate([base + h * 64 + np.arange(64), base + h * 64 + perm])
            tiles.append(_ptile(w_in[:, cols], 8))
    sh["wqk"] = np.ascontiguousarray(np.stack(tiles, 0))
    sh["wv"] = _ptile(w_in[:, 1024:1536], 8)
    sh["wu"] = np.ascontiguousarray(np.stack([_ptile(w_in[:, 1536 + g * 128:1536 + (g + 1) * 128], 8) for g in range(4)], 0))
    sh["wg"] = np.ascontiguousarray(np.stack([_ptile(w_in[:, 2048 + t * 128:2048 + (t + 1) * 128], 8) for t in range(16)], 0))
    sh["wba"] = _ptile(f("w_branch_attn")[0], 4)
    sh["wbp"] = _ptile(f("w_branch_pool")[0], 4)
    sh["wpool"] = np.ascontiguousarray(f("w_pool")[0].transpose(1, 0, 2))
    sh["wout"] = _ptile(f("w_out")[0], 8)
    wg_ = f("w_ffn_gate")[0].reshape(8, 128, NF, 128).transpose(2, 1, 0, 3)
    wu_ = f("w_ffn_up")[0].reshape(8, 128, NF, 128).transpose(2, 1, 0, 3)
    sh["wgu"] = np.ascontiguousarray(np.stack([wg_, wu_], 2))
    sh["wd"] = np.ascontiguousarray(f("w_ffn_down")[0].reshape(NF, 128, 8, 128).transpose(2, 1, 0, 3))
    sh["lnp"] = np.ascontiguousarray(np.stack(
        [np.broadcast_to(f(k)[0][None, :], (128, 1024)) for k in ("ln1_g", "ln1_b", "ln2_g", "ln2_b")], 0))
    small = np.zeros((128, 128), np.float32)
    small[:, 0:16] = np.broadcast_to((1.0 / np.arange(1, 17, dtype=np.float32))[None, :], (128, 16))
    small[:, 16:32] = f("b_gate")[0].reshape(16, 128).T
    small[:, 32:36] = f("pool_scale")[0].reshape(4, 128).T
    small[:, 36:58] = f("conv_b")[0].reshape(NF, 128).T
    small[:, 58:124] = f("conv_w")[0].reshape(3, NF, 128).transpose(2, 1, 0).reshape(128, 66)
    sh["small"] = small
    return sh


_CACHE = {}


def _get_nc():
    if "nc" not in _CACHE:
        _CACHE["nc"] = build_program()[0]
    return _CACHE["nc"]


def kernel(**inputs):
    x = np.asarray(inputs["x"], dtype=np.float32)
    sh = prep_shared(inputs)
    in_maps = []
    for b in range(8):
        m = dict(sh)
        m["x"] = np.ascontiguousarray(x[b])
        m["xT"] = np.ascontiguousarray(x[b].T)
        in_maps.append(m)
    nc = _get_nc()
    res = run_bass_kernel_spmd(nc, in_maps, core_ids=list(range(8)))
    return np.stack([np.asarray(r["out"], dtype=np.float32) for r in res.results], 0)
```

```python
import bisect
import numpy as np
import concourse.bass as bass
import concourse.mybir as mybir
from concourse.bass_utils import run_bass_kernel_spmd

F32 = mybir.dt.float32
BF16 = mybir.dt.bfloat16
U8 = mybir.dt.uint8
AF = mybir.ActivationFunctionType
ALU = mybir.AluOpType
AX = mybir.AxisListType

S_ = 2048
D = 1024
H = 8
NBLK = 8
DFF = 2816
NF = 22
ALPHA = float(2.0 ** 0.25)
EPS = 1e-5
BIGNEG = -262144.0
KB = 1024
SYNC_ALL_SAME_ENGINE = True
SAME_ENG_RAW_DIST = 3


class _Op:
    __slots__ = ("eng", "fn", "reads", "writes", "deps", "sig", "sem", "ticket", "inc",
                 "idx", "eidx", "is_dma", "waits", "clock")


class _IMap:
    def __init__(self, size):
        self.b = [0, size]
        self.s = [[None, {}]]

    def _split(self, x):
        i = bisect.bisect_left(self.b, x)
        if i < len(self.b) and self.b[i] == x:
            return
        st = self.s[i - 1]
        self.b.insert(i, x)
        self.s.insert(i, [st[0], dict(st[1])])

    def segs(self, lo, hi):
        self._split(lo)
        self._split(hi)
        i = bisect.bisect_left(self.b, lo)
        j = bisect.bisect_left(self.b, hi)
        return self.s[i:j]


class Sched:
    COMPUTE = ("pe", "act", "dve", "pool")

    def __init__(self, nc):
        self.nc = nc
        self.ops = []
        self.tens = {}
        self.maps = {}
        self.keys = {}

    def reg(self, handle, space, base, size):
        rowlen = 1
        for d in list(handle.shape)[1:]:
            rowlen *= int(d)
        esize = 2 if handle.dtype == BF16 else (1 if handle.dtype == U8 else 4)
        self.tens[handle.name] = (space, base, esize, rowlen)
        if space not in self.maps:
            self.maps[space] = _IMap(size)

    def _rng(self, x):
        if isinstance(x, (str, tuple)):
            return [("key", x, 0, 0)]
        space, base, esize, rowlen = self.tens[x.tensor.name]
        if space != "sb":
            return [("iv", space, 0, 2048)]
        dims = [(int(st), int(n)) for (st, n) in list(x.ap)[1:] if int(n) > 1]
        col = int(x.offset) % rowlen
        ivs = [(col, col + 1)]
        for (st, n) in sorted(dims, key=lambda d: abs(d[0])):
            if st < 0:
                ivs = [(a + st * (n - 1), b + st * (n - 1)) for (a, b) in ivs]
                st = -st
            span = ivs[-1][1] - ivs[0][0]
            if len(ivs) == 1 and st <= span:
                ivs = [(ivs[0][0], ivs[0][1] + st * (n - 1))]
            elif len(ivs) * n <= 64:
                ivs = [(a + i * st, b + i * st) for i in range(n) for (a, b) in ivs]
                ivs.sort()
                m = [ivs[0]]
                for (a, b) in ivs[1:]:
                    if a <= m[-1][1]:
                        m[-1] = (m[-1][0], max(m[-1][1], b))
                    else:
                        m.append((a, b))
                ivs = m
            else:
                ivs = [(ivs[0][0], ivs[-1][1] + st * (n - 1))]
        return [("iv", space, base + a * esize, base + b * esize) for (a, b) in ivs]

    def op(self, eng, fn, reads=(), writes=()):
        o = _Op()
        o.eng, o.fn, o.is_dma = eng, fn, False
        o.reads = [q for r in reads for q in self._rng(r)]
        o.writes = [q for w in writes for q in self._rng(w)]
        o.idx = len(self.ops)
        self.ops.append(o)
        return o

    def dma(self, queue, fn, reads=(), writes=()):
        o = self.op(queue, fn, reads, writes)
        o.is_dma = True
        return o

    def _states(self, r):
        if r[0] == "key":
            st = self.keys.get(r[1])
            if st is None:
                st = [None, {}]
                self.keys[r[1]] = st
            return [st]
        return self.maps[r[1]].segs(r[2], r[3])

    def _analyse(self):
        ecount = {}
        for o in self.ops:
            o.eidx = ecount.get(o.eng, 0)
            ecount[o.eng] = o.eidx + 1
            deps = {}
            rk = ("dma", o.idx) if o.is_dma else o.eng
            for r in o.reads:
                for st in self._states(r):
                    w = st[0]
                    if w is not None:
                        deps[w.idx] = (w, True)
            for wv in o.writes:
                for st in self._states(wv):
                    w = st[0]
                    if w is not None and w.idx not in deps:
                        deps[w.idx] = (w, False)
                    for rd in st[1].values():
                        if rd.idx not in deps:
                            deps[rd.idx] = (rd, False)
            for r in o.reads:
                for st in self._states(r):
                    st[1][rk] = o
            for wv in o.writes:
                for st in self._states(wv):
                    st[0] = o
                    st[1].clear()
            need = []
            for d, raw in deps.values():
                if d is o:
                    continue
                if d.is_dma or o.is_dma or d.eng != o.eng:
                    need.append(d)
                elif o.eng != "pe" and (SYNC_ALL_SAME_ENGINE or (raw and (o.eidx - d.eidx) < SAME_ENG_RAW_DIST)):
                    need.append(d)
            o.deps = need
            o.reads = o.writes = None
        for o in self.ops:
            o.sig = o.is_dma
        for o in self.ops:
            for d in o.deps:
                d.sig = True

    def emit(self, sems):
        self._analyse()
        nc = self.nc
        known = {}
        counters = {e: 0 for e in self.COMPUTE}
        dma_n = {}
        dma_last = {}
        for o in self.ops:
            kn = known.setdefault(o.eng, {})
            waits = []

            def need(semkey, val, clock):
                if kn.get(semkey, 0) >= val:
                    return
                waits.append((semkey, val))
                for k2, v2 in clock.items():
                    if kn.get(k2, 0) < v2:
                        kn[k2] = v2

            for d in sorted(o.deps, key=lambda d: d.idx):
                need(d.sem, d.ticket, d.clock)
            if o.is_dma:
                pool = sems["dma:" + o.eng]
                n = dma_n.get(o.eng, 0)
                dma_n[o.eng] = n + 1
                semkey = ("dma", o.eng, n % len(pool))
                prev = dma_last.get(semkey, 0)
                if prev:
                    need(semkey, prev, {semkey: prev})
                o.sem, o.ticket, o.inc = semkey, prev + 16, 16
                dma_last[semkey] = o.ticket
            elif o.sig:
                counters[o.eng] += 1
                o.sem, o.ticket, o.inc = o.eng, counters[o.eng], 1
            o.waits = waits
            if o.sig:
                c = dict(kn)
                c[o.sem] = o.ticket
                o.clock = c
            o.deps = None

        def semh(key):
            if isinstance(key, tuple):
                return sems["dma:" + key[1]][key[2]]
            return sems[key]

        per_eng = {}
        for o in self.ops:
            per_eng.setdefault(o.eng, []).append(o)

        def run(engname, e):
            for o in per_eng.get(engname, []):
                for (sk, val) in o.waits:
                    e.wait_ge(semh(sk), val)
                inst = o.fn(e)
                if o.sig:
                    inst.then_inc(semh(o.sem), o.inc)

        with nc.Block() as block:
            @block.tensor
            def _(e):
                run("pe", e)

            @block.scalar
            def _(e):
                run("act", e)

            @block.vector
            def _(e):
                run("dve", e)

            @block.gpsimd
            def _(e):
                run("pool", e)

            @block.sync
            def _(e):
                run("sp", e)
        return dict(n_ops=len(self.ops), n_waits=sum(len(o.waits) for o in self.ops),
                    counters=counters, per_eng={k: len(v) for k, v in per_eng.items()})


def build_program(stop_after=None, dbg=None):
    nc = bass.Bass("TRN2", target_bir_lowering=False)
    es_tensors = []

    def din(name, shape):
        return nc.dram_tensor(name, list(shape), F32, kind="ExternalInput").ap()

    xT_d = din("xT", [D, S_])
    x_d = din("x", [S_, D])
    wqk_d = din("wqk", [16, 128, 8, 128])
    wv_d = din("wv", [128, 8, 512])
    wu_d = din("wu", [4, 128, 8, 128])
    wg_d = din("wg", [16, 128, 8, 128])
    wba_d = din("wba", [128, 4, 1024])
    wbp_d = din("wbp", [128, 4, 1024])
    wpool_d = din("wpool", [128, 4, 128])
    wout_d = din("wout", [128, 8, 1024])
    wgu_d = din("wgu", [NF, 128, 2, 8, 128])
    wd_d = din("wd", [8, 128, NF, 128])
    lnp_d = din("lnp", [4, 128, 1024])
    ropeC_d = din("ropeC", [128, S_])
    ropeS_d = din("ropeS", [128, S_])
    ident_d = din("ident", [128, 128])
    mAB_d = din("mAB", [128, 512])
    oh_d = din("oh", [8, S_])
    am_d = din("am", [128, 8, 64])
    small_d = din("small", [128, 128])
    out_d = nc.dram_tensor("out", [S_, D], F32, kind="ExternalOutput").ap()
    dbg_d = None
    if dbg is not None:
        dbg_d = nc.dram_tensor("dbg", list(dbg), F32, kind="ExternalOutput").ap()

    ARENA = 206 * KB
    arena_cm = nc.sbuf_tensor("arena", [128, ARENA], U8)
    arena = arena_cm.__enter__()
    abase = int(nc.lookup_mloc(arena).addr)
    S = Sched(nc)

    def T(name, shape, dt, off):
        h = nc.alloc_sbuf_tensor_at(name, list(shape), dt, offset=abase + off)
        S.reg(h, "sb", abase + off, 256 * KB)
        return h

    o = 0
    identb = T("identb", [128, 128], BF16, o); o += 256
    identf = T("identf", [128, 128], F32, o); o += 512
    mAB = T("mABb", [128, 512], BF16, o); o += 1024
    AM = T("AM", [128, 8, 64], F32, o); o += 2048
    small = T("small", [128, 128], F32, o); o += 512
    halo = T("halo", [128, NF, 2], F32, o); o += 256
    stt_ = T("lnst", [128, 8, 12], F32, o); o += 512
    mv_ = T("lnmv", [128, 8, 4], F32, o); o += 128
    assert o <= 6 * KB
    RA = 6 * KB
    RB = RA + 32 * KB
    RC = RB + 33 * KB
    RX = RC + 33 * KB
    assert RX + 100 * KB <= ARENA
    EXTRA = RX + 90 * KB
    rcw = small[:, 0:16]
    bgate = small[:, 16:32]
    pscale = small[:, 32:36]
    convb = small[:, 36:58]
    convw = small[:, 58:124].rearrange("p (f k) -> p f k", k=3)

    xTb = T("xTb", [128, 8, S_], BF16, RA)
    x1 = T("x1", [128, 8, 1024], F32, RA)
    U = T("U", [128, 4, 2064], F32, RB)
    qh = [T(f"q{h}", [128, S_], BF16, RB + h * 4 * KB) for h in range(H)]
    mixinT = T("mixinT", [128, 8, S_], BF16, RB)
    hTa = T("hTa", [128, 16, 1024], BF16, RB)
    PA = T("PA", [128, 2064], F32, RC)
    PB = T("PB", [128, 2064], F32, RC + 8256)
    PL = T("PL", [128, 4, S_], BF16, RC + 16512)
    tmpf = T("tmpf", [128, 16], F32, RC + 16512 + 16384)
    kh = [T(f"k{h}", [128, S_], BF16, RC + h * 4 * KB) for h in range(H)]
    x1Tb = T("x1Tb", [128, 8, 1024], BF16, RC)
    wout = T("wout", [128, 8, 1024], BF16, RC + 16 * KB)
    hTb = T("hTb", [128, 6, 1024], BF16, RC + 16 * KB)
    poolinT = T("poolinT", [128, 4, S_], BF16, RX)
    attnT = T("attnT", [128, 4, S_], BF16, RX + 16 * KB)
    ropeC = T("ropeC", [128, S_], F32, RX + 16 * KB)
    ropeS = T("ropeS", [128, S_], F32, RX + 24 * KB)
    Vp = T("Vp", [128, 16, 8, 128], BF16, RX + 32 * KB)
    M0 = RX + 64 * KB
    wbp = T("wbp", [128, 4, 1024], BF16, M0)
    wba = T("wba", [128, 4, 1024], BF16, M0 + 15 * KB)
    wu = T("wu", [128, 4, 8, 128], BF16, M0)
    wpool = T("wpoolb", [128, 4, 128], BF16, M0 + 8 * KB)
    wqk = [T(f"wqk{i}", [128, 8, 128], BF16, M0 + i * 2 * KB) for i in range(3)]
    wv = T("wv", [128, 8, 512], BF16, M0 + 9 * KB)
    t1 = [T(f"t1_{i}", [128, 512], F32, M0 + 17 * KB + i * 2 * KB) for i in range(2)]
    t2 = [T(f"t2_{i}", [128, 512], F32, M0 + 21 * KB + i * 2 * KB) for i in range(2)]
    biasq = T("biasq", [128, 16, 128], BF16, M0)
    km32 = T("km32", [128, 64], F32, M0 + 25 * KB)
    kmb = T("kmb", [128, 64], BF16, M0 + 25 * KB + 256)
    junk = T("junk", [128, 256], BF16, M0 + 25 * KB + 512)
    tsel = [T(f"tsel{i}", [128, 64], F32, M0 + 5 * KB + 512 + i * 256) for i in range(2)]
    gm = [T(f"gm{i}", [128, 64], F32, M0 + 4 * KB + 512 + i * 256) for i in range(2)]
    srt = [T(f"srt{i}", [128, 64], F32, M0 + 5 * KB + i * 256) for i in range(2)]
    PT = [T(f"PT{i}", [128, 512], BF16, M0 + 8 * KB + i * KB) for i in range(5)]
    rl = [T(f"rl{i}", [128, 256], F32, M0 + 13 * KB + i * KB) for i in range(2)]
    wga = [T(f"wga{i}", [128, 8, 128], BF16, EXTRA + i * 2 * KB) for i in range(2)]
    wgp = [T(f"wgp{i}", [128, 8, 128], BF16, EXTRA + 4 * KB + i * 2 * KB) for i in range(2)]
    sga = [T(f"sga{i}", [128, 512], F32, RX + 32 * KB + i * 2 * KB) for i in range(2)]
    sgp = [T(f"sgp{i}", [128, 512], F32, RX + 32 * KB + 4 * KB + i * 2 * KB) for i in range(2)]
    m1 = [T(f"m1_{i}", [128, 512], F32, RX + 32 * KB + 8 * KB + i * 2 * KB) for i in range(2)]
    m2 = [T(f"m2_{i}", [128, 512], F32, RX + 32 * KB + 12 * KB + i * 2 * KB) for i in range(2)]
    x1H1 = T("x1H1", [128, 8, 1024], F32, RX)
    x1TH1 = T("x1TH1", [128, 8, 1024], BF16, RX + 32 * KB)
    WS = RX + 48 * KB

    pst = T("pst", [128, 8, 8, 6], F32, RX + 60 * KB)

    def hTf(f):
        return hTa[:, f, :] if f < 16 else hTb[:, f - 16, :]
    wgu = [T("wgu0", [128, 2, 8, 128], BF16, RX + 64 * KB + 19 * KB), T("wgu1", [128, 2, 8, 128], BF16, WS + 4 * KB)]
    wd = [T(f"wd{i}", [128, NF, 128], BF16, WS + i * 5632) for i in range(2)]
    LNP = RX + 56 * KB
    ln1g = T("ln1g", [128, 1024], F32, LNP)
    ln1b = T("ln1b", [128, 1024], F32, LNP + 4 * KB)
    WK = RX + 64 * KB
    xres = [T(f"xres{i}", [128, 1024], F32, WK + i * 4 * KB) for i in range(3)]
    x1b = [T(f"x1b{i}", [128, 1024], BF16, WK + 12 * KB + i * 2 * KB) for i in range(2)]
    abuf = [T("abuf0", [128, 1032], F32, RC + 28 * KB), T("abuf1", [128, 1032], F32, RX + 89 * KB)]
    cbuf = [T(f"cbuf{i}", [128, 512], F32, RX + 94 * KB + i * 2 * KB) for i in range(2)]
    gbuf = [T("gbuf0", [128, 512], F32, RX + 98 * KB), T("gbuf1", [128, 512], F32, WK + 17 * KB)]
    oT = [T(f"oT{i}", [128, 512], F32, WK + 12 * KB + i * 2 * KB) for i in range(2)]
    ln2g = T("ln2g", [128, 1024], F32, WK)
    ln2b = T("ln2b", [128, 1024], F32, WK + 4 * KB)
    assert WK + 25 * KB <= RX + 90 * KB

    ps = []
    for i in range(6):
        p = nc.alloc_psum_tensor(f"ps{i}", [128, 512], F32)
        S.reg(p, f"ps{i}", 0, 4096)
        ps.append(p)
    psTs = []
    for i in range(2):
        p = nc.alloc_psum_tensor(f"psT{i}", [128, 1024], BF16)
        S.reg(p, f"psT{i}", 0, 4096)
        psTs.append(p)
    pst_rr = [0]

    def npsT():
        b = psTs[pst_rr[0] % 2]
        pst_rr[0] += 1
        return b

    sem_cms = []

    def mk_sem(name):
        cm = nc.semaphore(name)
        sem_cms.append(cm)
        return cm.__enter__()

    sems = {k: mk_sem("s_" + k) for k in ("pe", "act", "dve", "pool")}
    sems["dma:sp"] = [mk_sem(f"dsp{i}") for i in range(12)]
    sems["dma:pool"] = [mk_sem(f"dpl{i}") for i in range(16)]

    def ld_cast(dst, src):
        S.dma("pool", lambda e: e.dma_start(out=dst, in_=src), writes=[dst])

    def ld(dst, src):
        S.dma("sp", lambda e: e.dma_start(out=dst, in_=src), writes=[dst])

    def mm(out, lhsT, rhs, start, stop):
        S.op("pe", lambda e: e.matmul(out, lhsT, rhs, start=start, stop=stop),
             reads=[lhsT, rhs], writes=[out])

    def tr(out, in_, ident):
        S.op("pe", lambda e: e.transpose(out, in_, ident), reads=[in_, ident], writes=[out])

    def act(out, in_, func, bias=None, scale=None):
        rd = [in_]
        kw = {}
        if bias is not None:
            kw["bias"] = bias
            if not isinstance(bias, float):
                rd.append(bias)
        if scale is not None:
            kw["scale"] = scale
            if not isinstance(scale, float):
                rd.append(scale)
        S.op("act", lambda e: e.activation(out=out, in_=in_, func=func, **kw), reads=rd, writes=[out])

    def tt(eng, out, in0, in1, op):
        S.op(eng, lambda e: e.tensor_tensor(out=out, in0=in0, in1=in1, op=op), reads=[in0, in1], writes=[out])

    def ts(eng, out, in0, s1, s2, op0, op1=None):
        rd = [in0] + [s for s in (s1, s2) if s is not None and not isinstance(s, float)]
        if op1 is None:
            S.op(eng, lambda e: e.tensor_scalar(out=out, in0=in0, scalar1=s1, scalar2=s2, op0=op0),
                 reads=rd, writes=[out])
        else:
            S.op(eng, lambda e: e.tensor_scalar(out=out, in0=in0, scalar1=s1, scalar2=s2, op0=op0, op1=op1),
                 reads=rd, writes=[out])

    def stt(eng, out, in0, scalar, in1, op0, op1):
        rd = [in0, in1] + ([] if isinstance(scalar, float) else [scalar])
        S.op(eng, lambda e: e.scalar_tensor_tensor(out=out, in0=in0, scalar=scalar, in1=in1, op0=op0, op1=op1),
             reads=rd, writes=[out])

    def memset(eng, ap, val):
        S.op(eng, lambda e: e.memset(ap, val), writes=[ap])

    def cp(eng, out, in_):
        S.op(eng, lambda e: e.tensor_copy(out, in_), reads=[in_], writes=[out])

    def dump(ap_sb, rows, cols, r0=0, c0=0):
        S.dma("sp", lambda e: e.dma_start(out=dbg_d[r0:r0 + rows, c0:c0 + cols], in_=ap_sb),
              reads=[ap_sb], writes=["dbg"])

    def finish():
        S.op("sp", lambda e: e.nop(), reads=["out", "dbg"])
        info = S.emit(sems)
        for cm in reversed(sem_cms):
            cm.__exit__(None, None, None)
        arena_cm.__exit__(None, None, None)
        return nc, info

    ld(identf[:], ident_d)
    ld(AM[:], am_d)
    ld(small[:], small_d)
    xT_v = xT_d.rearrange("(dt p) t -> p dt t", p=128)
    ld_cast(xTb[:, :, 0:512], xT_v[:, :, 0:512])
    for g in range(4):
        ld_cast(wu[:, g, :, :], wu_d[g])
    for c in range(1, 4):
        ld_cast(xTb[:, :, c * 512:(c + 1) * 512], xT_v[:, :, c * 512:(c + 1) * 512])
    ld_cast(identb[:], ident_d)
    ld_cast(mAB[:], mAB_d)
    MA = mAB[:, 0:256]
    MB = mAB[:, 256:512]

    bank_rr = [0]

    def nbank():
        b = ps[bank_rr[0] % 6]
        bank_rr[0] += 1
        return b

    ld_cast(wpool[:], wpool_d)
    memset("dve", U[:, :, 0:16], 0.0)
    memset("dve", PA[:, 0:16], 0.0)
    memset("dve", PB[:, 0:16], 0.0)
    for c in range(4):
        for g in range(4):
            bk = nbank()
            for dt in range(8):
                mm(bk[:, :], wu[:, g, dt, :], xTb[:, dt, c * 512:(c + 1) * 512], dt == 0, dt == 7)
            act(U[:, g, 16 + c * 512:16 + (c + 1) * 512], bk[:, :], AF.Copy)
    ld_cast(wv[:], wv_d)
    memset("pool", Vp[:].rearrange("p t h c -> p (t h) c")[:, :, 64:128], 1.0)
    for t_ in range(16):
        bk = nbank()
        for dt in range(8):
            mm(bk[:, :], xTb[:, dt, t_ * 128:(t_ + 1) * 128], wv[:, dt, :], dt == 0, dt == 7)
        act(Vp[:, t_, :, 0:64], bk[:, :].rearrange("p (h c) -> p h c", c=64), AF.Copy)
    for w in range(3):
        ld_cast(wqk[w][:], wqk_d[w])
    for g in range(4):
        src = U[:, g, :]
        cur = src
        bufs = [PA, PB]
        for lev in range(g + 1):
            dst = bufs[lev % 2]
            sh = 2 ** lev
            tt("dve", dst[:, 16:2064], cur[:, 16:2064], cur[:, 16 - sh:2064 - sh], ALU.add)
            cur = dst[:, :]
        w = 2 ** (g + 1)
        stt("dve", PL[:, g, :], cur[:, 16:2064], 1.0 / w, src[:, 16:2064], ALU.mult, ALU.subtract)
        tt("dve", tmpf[:, 0:w - 1], cur[:, 16:16 + w - 1], rcw[:, 0:w - 1], ALU.mult)
        tt("dve", PL[:, g, 0:w - 1], tmpf[:, 0:w - 1], src[:, 16:16 + w - 1], ALU.subtract)
    for g in range(4):
        for c in range(4):
            bk = nbank()
            mm(bk[:, :], wpool[:, g, :], PL[:, g, c * 512:(c + 1) * 512], True, True)
            act(poolinT[:, g, c * 512:(c + 1) * 512], bk[:, :], AF.Copy, scale=pscale[:, g:g + 1])
    if stop_after == "pool":
        for g in range(4):
            cp("dve", t1[0][:, 0:512], poolinT[:, g, 0:512])
            dump(t1[0][:, 0:512], 128, 512, r0=g * 128)
        return finish()

    ld(ropeC[:], ropeC_d)
    ld(ropeS[:], ropeS_d)
    for h in range(H):
        ld_cast(kh[h][64:72, :], oh_d)
    pend = []
    ui = 0
    def kmean_head(h):
        for n in range(NBLK):
            S.op("act", (lambda h, n: lambda e: e.activation(
                out=junk[0:64, :], in_=kh[h][0:64, n * 256:(n + 1) * 256], func=AF.Copy,
                accum_out=km32[0:64, h * 8 + n:h * 8 + n + 1]))(h, n),
                reads=[kh[h][0:64, n * 256:(n + 1) * 256]],
                writes=[junk[0:64, :], km32[0:64, h * 8 + n:h * 8 + n + 1]])

    for w in range(16):
        if w - 1 >= 8:
            pass
        dstT = qh[w] if w < 8 else kh[w - 8]
        for c in range(4):
            bk = nbank()
            for dt in range(8):
                mm(bk[:, :], wqk[w % 3][:, dt, :], xTb[:, dt, c * 512:(c + 1) * 512], dt == 0, dt == 7)
            s = ui % 2
            ui += 1
            cs = slice(c * 512, (c + 1) * 512)
            tt("dve", t1[s][0:64, :], bk[0:64, :], ropeC[0:64, cs], ALU.mult)
            tt("dve", t2[s][0:64, :], bk[64:128, :], ropeS[64:128, cs], ALU.mult)
            if pend:
                a, b_, d_ = pend.pop()
                tt("pool", d_, a, b_, ALU.add)
            pend.append((t1[s][0:64, :], t2[s][0:64, :], dstT[0:64, cs]))
        if w + 3 < 16:
            ld_cast(wqk[w % 3][:], wqk_d[w + 3])
        if w >= 9:
            kmean_head(w - 9)
    a, b_, d_ = pend.pop()
    tt("pool", d_, a, b_, ALU.add)
    kmean_head(7)
    if stop_after == "qkv":
        cp("dve", t1[0][0:64, :], qh[0][0:64, 0:512])
        dump(t1[0][0:64, :], 64, 512, r0=0)
        cp("dve", t1[1][0:72, :], kh[1][0:72, 512:1024])
        dump(t1[1][0:72, :], 72, 512, r0=64)
        cp("dve", t2[0][:, 0:512].rearrange("p (h c) -> p h c", c=64), Vp[:, 3, :, 0:64])
        dump(t2[0][:, 0:512], 128, 512, r0=136)
        return finish()

    ts("dve", kmb[0:64, :], km32[0:64, :], 1.0 / 256.0, None, ALU.mult)
    if stop_after == "g1":
        cp("dve", t1[0][0:64, 0:64], km32[0:64, :])
        dump(t1[0][0:64, 0:64], 64, 64)
        return finish()
    memset("dve", biasq[:, :, :], 0.0)
    for qt in range(16):
        G = ps[4 + qt % 2][:, 0:64]
        for h in range(H):
            mm(G[:, h * 8:(h + 1) * 8], qh[h][0:64, qt * 128:(qt + 1) * 128], kmb[0:64, h * 8:(h + 1) * 8], True, True)
        s = qt % 2
        tt("dve", gm[s][:, :], G, AM[:, qt // 2, :], ALU.add)
        for h in range(H):
            S.op("dve", (lambda s, h: lambda e: e.max(out=srt[s][:, h * 8:(h + 1) * 8], in_=gm[s][:, h * 8:(h + 1) * 8]))(s, h),
                 reads=[gm[s][:, h * 8:(h + 1) * 8]], writes=[srt[s][:, h * 8:(h + 1) * 8]])
        b3 = srt[s][:, 3:4]
        thr = bass.AP(b3.tensor, b3.offset, [list(b3.ap[0]), [8, 8], [0, 8]])
        S.op("dve", (lambda s, thr: lambda e: e.tensor_tensor(
            out=tsel[s][:, :].rearrange("p (h n) -> p h n", n=8), in0=gm[s][:, :].rearrange("p (h n) -> p h n", n=8),
            in1=thr, op=ALU.is_lt))(s, thr), reads=[gm[s][:, :], srt[s][:, :]], writes=[tsel[s][:, :]])
        ts("dve", biasq[:, qt, 64:128], tsel[s][:, :], BIGNEG, None, ALU.mult)
    if stop_after == "g2":
        cp("dve", t1[0][:, 0:128], biasq[:, 9, :])
        dump(t1[0][:, 0:128], 128, 128)
        return finish()
    def bias_rows(h):
        for c in range(4):
            pT = npsT()
            for i in range(4):
                qt = c * 4 + i
                tr(pT[0:72, i * 128:(i + 1) * 128], biasq[:, qt, h * 8:h * 8 + 72], identb[:])
            act(qh[h][64:72, c * 512:(c + 1) * 512], pT[64:72, 0:512], AF.Copy)

    if stop_after == "gate":
        for h in range(H):
            bias_rows(h)
    else:
        bias_rows(0)
        bias_rows(1)
    if stop_after == "gate":
        cp("dve", t1[0][0:64, :], qh[2][0:64, 1024:1536])
        cp("dve", t1[0][64:72, :], qh[2][64:72, 1024:1536])
        dump(t1[0][0:72, :], 72, 512, r0=0)
        cp("dve", t1[1][0:64, :], qh[5][0:64, 1536:2048])
        cp("dve", t1[1][64:72, :], qh[5][64:72, 1536:2048])
        dump(t1[1][0:72, :], 72, 512, r0=72)
        return finish()

    ld_cast(wba[:], wba_d)
    for j in range(2):
        ld_cast(wga[j][:], wg_d[j])
        ld_cast(wgp[j][:], wg_d[8 + j])
    items = []
    for h in range(H):
        for b in range(NBLK):
            for p in range(b + 1):
                items.append((h, b, p))
    SB = [ps[0], ps[1], ps[2], ps[3]]
    ACC = [ps[4], ps[5]]
    acc_i = -1

    def vl(h, kt):
        return Vp[:, kt, h, :]

    def pv(i):
        h, b, p = items[i]
        ai = acc_of[i]
        acc = ACC[ai % 2][:, 0:256]
        pt = PT[i % 5]
        mm(acc, vl(h, 2 * p), pt[:, 0:256], p == 0, False)
        mm(acc, vl(h, 2 * p + 1), pt[:, 256:512], False, p == b)
        if p == b:
            r = rl[ai % 2]
            S.op("dve", lambda e: e.reciprocal(out=r[0:64, :], in_=acc[64:128, :]),
                 reads=[acc[64:128, :]], writes=[r[0:64, :]])
            po = (h % 2) * 64
            tt("dve", attnT[po:po + 64, h // 2, b * 256:(b + 1) * 256], acc[0:64, :], r[0:64, :], ALU.mult)

    acc_of = []
    for i, (h, b, p) in enumerate(items):
        if p == 0:
            acc_i += 1
        acc_of.append(acc_i)
    for i, (h, b, p) in enumerate(items):
        if b == 2 and p == 0 and h + 2 < H:
            bias_rows(h + 2)
            if h + 2 == H - 1:
                ld_cast(wbp[:], wbp_d)
        sb = SB[i % 4]
        qs = qh[h][0:72, b * 256:(b + 1) * 256]
        for half in range(2):
            kt = 2 * p + half
            o_ = sb[:, half * 256:(half + 1) * 256]
            mm(o_, kh[h][0:72, kt * 128:(kt + 1) * 128], qs, True, p != b)
            if p == b:
                mm(o_, identb[:], MA if half == 0 else MB, False, True)
        act(PT[i % 5][:, :], sb[:, :], AF.Exp, scale=0.125)
        if i >= 3:
            pv(i - 3)
    pv(len(items) - 3)
    pv(len(items) - 2)
    pv(len(items) - 1)
    if stop_after == "attn":
        for ct in range(4):
            cp("dve", t1[0][:, 0:512], attnT[:, ct, 1536:2048])
            dump(t1[0][:, 0:512], 128, 512, r0=ct * 128)
        return finish()

    ld_cast(wout[:], wout_d)
    ld(ln1g[:], lnp_d[0])
    ld(ln1b[:], lnp_d[1])
    ui = 0
    for j in range(8):
        js = slice(j * 128, (j + 1) * 128)
        for tc in range(4):
            cs = slice(tc * 512, (tc + 1) * 512)
            b0, b1, b2, b3 = nbank(), nbank(), nbank(), nbank()
            for dt in range(8):
                mm(b2[:, :], wga[j % 2][:, dt, :], xTb[:, dt, cs], dt == 0, dt == 7)
            for dt in range(8):
                mm(b3[:, :], wgp[j % 2][:, dt, :], xTb[:, dt, cs], dt == 0, dt == 7)
            for ct in range(4):
                mm(b0[:, :], wba[:, ct, js], attnT[:, ct, cs], ct == 0, ct == 3)
            for g in range(4):
                mm(b1[:, :], wbp[:, g, js], poolinT[:, g, cs], g == 0, g == 3)
            s = ui % 2
            ui += 1
            act(sga[s][:, :], b2[:, :], AF.Sigmoid, bias=bgate[:, j:j + 1])
            act(sgp[s][:, :], b3[:, :], AF.Sigmoid, bias=bgate[:, 8 + j:9 + j])
            tt("dve", m1[s][:, :], b0[:, :], sga[s][:, :], ALU.mult)
            tt("dve", m2[s][:, :], b1[:, :], sgp[s][:, :], ALU.mult)
            tt("pool", mixinT[:, j, cs], m1[s][:, :], m2[s][:, :], ALU.add)
        if j + 2 < 8:
            ld_cast(wga[j % 2][:], wg_d[j + 2])
            ld_cast(wgp[j % 2][:], wg_d[8 + j + 2])
    if stop_after == "merge":
        for j in range(8):
            cp("dve", t1[0][:, 0:512], mixinT[:, j, 512:1024])
            dump(t1[0][:, 0:512], 128, 512, r0=j * 128)
        return finish()

    def ln_A(row, s):
        st = stt_[:, s, :]
        mv = mv_[:, s, :]
        S.op("dve", lambda e: e.bn_stats(out=st[:, 0:6], in_=row[:, 0:512]), reads=[row[:, 0:512]], writes=[st[:, 0:6]])
        S.op("dve", lambda e: e.bn_stats(out=st[:, 6:12], in_=row[:, 512:1024]), reads=[row[:, 512:1024]], writes=[st[:, 6:12]])
        S.op("dve", lambda e: e.bn_aggr(out=mv[:, 0:2], in_=st), reads=[st], writes=[mv[:, 0:2]])
        ts("dve", mv[:, 1:2], mv[:, 1:2], EPS, None, ALU.add)
        act(mv[:, 2:3], mv[:, 1:2], AF.Sqrt)

    def ln_B(row, s):
        mv = mv_[:, s, :]
        S.op("dve", lambda e: e.reciprocal(out=mv[:, 2:3], in_=mv[:, 2:3]), reads=[mv[:, 2:3]], writes=[mv[:, 2:3]])
        ts("dve", mv[:, 3:4], mv[:, 0:1], mv[:, 2:3], -1.0, ALU.mult, ALU.mult)
        act(row, row, AF.Identity, bias=mv[:, 3:4], scale=mv[:, 2:3])

    def ln_C(row, g_, b_):
        tt("dve", row, row, g_[:, :], ALU.mult)
        tt("pool", row, row, b_[:, :], ALU.add)

    x1_of = [x1, x1H1]
    x1T_of = [x1Tb, x1TH1]

    def p6_mm(T_):
        xr = xres[T_ % 3]
        ld(xr[:], x_d[T_ * 128:(T_ + 1) * 128, :])
        bks = []
        for hv in range(2):
            bk = nbank()
            for j in range(8):
                mm(bk[:, :], mixinT[:, j, T_ * 128:(T_ + 1) * 128], wout[:, j, hv * 512:(hv + 1) * 512], j == 0, j == 7)
            bks.append(bk)
        return bks

    def p6_R(T_, bks):
        xr = xres[T_ % 3]
        row = x1_of[T_ // 8][:, T_ % 8, :]
        for hv in range(2):
            stt("dve", row[:, hv * 512:(hv + 1) * 512], xr[:, hv * 512:(hv + 1) * 512], ALPHA, bks[hv][:, :], ALU.mult, ALU.add)
        ln_A(row, T_ % 4)

    def p6_B(T_):
        ln_B(x1_of[T_ // 8][:, T_ % 8, :], T_ % 4)

    def p6_C(T_):
        row = x1_of[T_ // 8][:, T_ % 8, :]
        ln_C(row, ln1g, ln1b)
        act(x1b[T_ % 2][:, :], row, AF.Copy)

    def p6_tr(T_):
        xb_ = x1b[T_ % 2]
        tl = T_ % 8
        dstT = x1T_of[T_ // 8]
        pT = npsT()
        for dt in range(8):
            tr(pT[:, dt * 128:(dt + 1) * 128], xb_[:, dt * 128:(dt + 1) * 128], identb[:])
        if T_ % 2 == 0:
            act(dstT[:, :, tl * 128:(tl + 1) * 128], pT[:, :].rearrange("p (a b) -> p a b", b=128), AF.Copy)
        else:
            cp("dve", dstT[:, :, tl * 128:(tl + 1) * 128], pT[:, :].rearrange("p (a b) -> p a b", b=128))

    def ln_A0(row, s):
        st = stt_[:, s, :]
        mv = mv_[:, s, :]
        S.op("dve", lambda e: e.bn_stats(out=st[:, 0:6], in_=row[:, 0:512]), reads=[row[:, 0:512]], writes=[st[:, 0:6]])
        S.op("dve", lambda e: e.bn_stats(out=st[:, 6:12], in_=row[:, 512:1024]), reads=[row[:, 512:1024]], writes=[st[:, 6:12]])
        S.op("dve", lambda e: e.bn_aggr(out=mv[:, 0:2], in_=st), reads=[st], writes=[mv[:, 0:2]])
        ts("dve", mv[:, 1:2], mv[:, 1:2], EPS, None, ALU.add)

    def ln2_steps(x1h, half, batched, tiles=range(8)):
        def store(tl):
            t_ = half * 8 + tl
            ln_C(x1h[:, tl, :], ln2g, ln2b)
            row = x1h[:, tl, :]
            S.dma("sp", (lambda t_, row: lambda e: e.dma_start(out=out_d[t_ * 128:(t_ + 1) * 128, :], in_=row))(t_, row),
                  reads=[row], writes=["out"])
        if batched:
            for tl in range(8):
                yield (lambda tl=tl: ln_A0(x1h[:, tl, :], tl))
            yield (lambda: act(mv_[:, :, 2:3], mv_[:, :, 1:2], AF.Sqrt))
            for step in range(8 + 1):
                def emit(step=step):
                    if step < 8:
                        ln_B(x1h[:, step, :], step)
                    if 0 <= step - 1 < 8:
                        store(step - 1)
                yield emit
        else:
            tl_ = list(tiles)
            n_ = len(tl_)
            for step in range(n_ + 2):
                def emit(step=step):
                    if step < n_:
                        ln_A(x1h[:, tl_[step], :], tl_[step] % 4)
                    if 0 <= step - 1 < n_:
                        ln_B(x1h[:, tl_[step - 1], :], tl_[step - 1] % 4)
                    if 0 <= step - 2 < n_:
                        store(tl_[step - 2])
                yield emit

    def p7_preload(which=(0, 1)):
        for f in which:
            ld_cast(wgu[f][:], wgu_d[f])

    def p7(half, xT, side=None):
        for _ in p7_gen(half, xT, side):
            pass

    def p7_gen(half, xT, side=None):
        ui = 0
        pend_tail = []

        def flush_tail():
            while pend_tail:
                s_, f_, cs_, bu_ = pend_tail.pop(0)
                act(gbuf[s_][:, :], cbuf[s_][:, :], AF.Gelu)
                tt("dve", hTf(f_)[:, cs_], gbuf[s_][:, :], bu_[:, :], ALU.mult)

        for f in range(NF):
            ab = abuf[f % 2]
            if half == 0:
                memset("dve", ab[:, 0:2], 0.0)
            else:
                cp("dve", ab[:, 0:2], halo[:, f, :])
            for c in range(2):
                cs = slice(c * 512, (c + 1) * 512)
                ba, bu = nbank(), nbank()
                for dt in range(8):
                    mm(ba[:, :], wgu[f % 2][:, 0, dt, :], xT[:, dt, cs], dt == 0, dt == 7)
                for dt in range(8):
                    mm(bu[:, :], wgu[f % 2][:, 1, dt, :], xT[:, dt, cs], dt == 0, dt == 7)
                s = ui % 2
                ui += 1
                act(ab[:, 2 + c * 512:2 + (c + 1) * 512], ba[:, :], AF.Copy)
                act(cbuf[s][:, :], ba[:, :], AF.Identity, bias=convb[:, f:f + 1], scale=convw[:, f, 2:3])
                stt("dve", cbuf[s][:, :], ab[:, 1 + c * 512:1 + (c + 1) * 512], convw[:, f, 1:2], cbuf[s][:, :], ALU.mult, ALU.add)
                stt("dve", cbuf[s][:, :], ab[:, c * 512:(c + 1) * 512], convw[:, f, 0:1], cbuf[s][:, :], ALU.mult, ALU.add)
                flush_tail()
                pend_tail.append((s, f, cs, bu))
            if half == 0:
                cp("dve", halo[:, f, :], ab[:, 1024:1026])
            if f + 2 < NF:
                ld_cast(wgu[f % 2][:], wgu_d[f + 2])
            if side is not None and f >= 2:
                nxt = next(side, None)
                if nxt is not None:
                    nxt()
            if f < NF - 1:
                yield f
        flush_tail()
        if side is not None:
            for nxt in side:
                nxt()
        yield NF - 1

    def p8(half, x1h, partial_stats=False):
        for j in range(2):
            ld_cast(wd[j][:], wd_d[j])

        def p8_mm(j):
            O = [nbank(), nbank()]
            for f in range(NF):
                for c in range(2):
                    mm(O[c][:, :], wd[j % 2][:, f, :], hTf(f)[:, c * 512:(c + 1) * 512], f == 0, f == NF - 1)
            return O

        ui = 0
        Onext = p8_mm(0)
        for j in range(8):
            O = Onext
            for c in range(2):
                act(oT[(ui + c) % 2][:, :], O[c][:, :], AF.Copy)
            if j + 1 < 8:
                Onext = p8_mm(j + 1)
            for c in range(2):
                s = ui % 2
                ui += 1
                tp = nbank()
                for i in range(4):
                    tr(tp[:, i * 128:(i + 1) * 128], oT[s][:, i * 128:(i + 1) * 128], identf[:])
                xv = x1h[:, c * 4:(c + 1) * 4, j * 128:(j + 1) * 128]
                stt("dve", xv, xv, ALPHA, tp[:, :].rearrange("p (a b) -> p a b", b=128), ALU.mult, ALU.add)
                if partial_stats:
                    for i in range(4):
                        tl = c * 4 + i
                        S.op("dve", (lambda tl, j: lambda e: e.bn_stats(out=pst[:, tl, j, :], in_=x1h[:, tl, j * 128:(j + 1) * 128]))(tl, j),
                             reads=[x1h[:, tl, j * 128:(j + 1) * 128]], writes=[pst[:, tl, j, :]])
            if j + 2 < 8:
                ld_cast(wd[j % 2][:], wd_d[j + 2])

    def ln2_tail(x1h, half):
        for tl in range(8):
            mv = mv_[:, tl, :]
            S.op("dve", (lambda tl, mv: lambda e: e.bn_aggr(out=mv[:, 0:2], in_=pst[:, tl, :, :].rearrange("p j k -> p (j k)")))(tl, mv),
                 reads=[pst[:, tl, :, :]], writes=[mv[:, 0:2]])
            ts("dve", mv[:, 1:2], mv[:, 1:2], EPS, None, ALU.add)
        act(mv_[:, :, 2:3], mv_[:, :, 1:2], AF.Sqrt)
        S.op("dve", lambda e: e.reciprocal(out=mv_[:, :, 2:3], in_=mv_[:, :, 2:3]), reads=[mv_[:, :, 2:3]], writes=[mv_[:, :, 2:3]])
        for tl in range(8):
            row = x1h[:, tl, :]
            mv = mv_[:, tl, :]
            t_ = half * 8 + tl
            stt("dve", row, row, mv[:, 0:1], ln2g[:, :], ALU.subtract, ALU.mult)
            stt("dve", row, row, mv[:, 2:3], ln2b[:, :], ALU.mult, ALU.add)
            S.dma("sp", (lambda t_, row: lambda e: e.dma_start(out=out_d[t_ * 128:(t_ + 1) * 128, :], in_=row))(t_, row),
                  reads=[row], writes=["out"])

    def p8_last(half, x1h):
        seq = [(c, j) for c in range(2) for j in range(8)]
        for k in range(2):
            ld_cast(wd[k][:], wd_d[seq[k][1]])

        def mm_(k):
            c, j = seq[k]
            O = nbank()
            for f in range(NF):
                mm(O[:, :], wd[k % 2][:, f, :], hTf(f)[:, c * 512:(c + 1) * 512], f == 0, f == NF - 1)
            return O

        side = None
        Onext = mm_(0)
        for k, (c, j) in enumerate(seq):
            O = Onext
            s = k % 2
            act(oT[s][:, :], O[:, :], AF.Copy)
            if k + 1 < len(seq):
                Onext = mm_(k + 1)
            tp = nbank()
            for i in range(4):
                tr(tp[:, i * 128:(i + 1) * 128], oT[s][:, i * 128:(i + 1) * 128], identf[:])
            xv = x1h[:, c * 4:(c + 1) * 4, j * 128:(j + 1) * 128]
            stt("dve", xv, xv, ALPHA, tp[:, :].rearrange("p (a b) -> p a b", b=128), ALU.mult, ALU.add)
            if k + 2 < len(seq):
                ld_cast(wd[k % 2][:], wd_d[seq[k + 2][1]])
            if c == 1 and j == 0:
                side = ln2_steps(x1h, half, False, tiles=range(0, 4))
            if side is not None:
                nxt = next(side, None)
                if nxt is not None:
                    nxt()
        for nxt in side:
            nxt()
        for emit in ln2_steps(x1h, half, False, tiles=range(4, 8)):
            emit()

    order = list(range(16))
    NT = 16
    pre = {0: p6_mm(order[0]), 1: p6_mm(order[1])}

    p7_preload()
    g7 = p7_gen(0, x1Tb)
    for step in range(NT + 3):
        if step + 2 < NT:
            pre[step + 2] = p6_mm(order[step + 2])
        if step < NT:
            p6_R(order[step], pre.pop(step))
        if 0 <= step - 1 < NT:
            p6_B(order[step - 1])
        if 0 <= step - 2 < NT:
            p6_C(order[step - 2])
        if 0 <= step - 3 < NT:
            p6_tr(order[step - 3])
        if step >= NT - 2:
            next(g7, None)
    if stop_after == "ln1":
        for tl in range(4):
            dump(x1[:, tl, :], 128, 1024, r0=tl * 128)
        for dt in range(2):
            cp("dve", xres[0][:, :], x1Tb[:, dt, :])
            dump(xres[0][:, :], 128, 1024, r0=512 + dt * 128)
        return finish()
    ld(ln2g[:], lnp_d[2])
    ld(ln2b[:], lnp_d[3])
    for _ in g7:
        pass
    if stop_after == "ffn_h":
        for f in range(4):
            cp("dve", xres[0][:, :], hTf(f * 5))
            dump(xres[0][:, :], 128, 1024, r0=f * 128)
        return finish()
    p7_preload((0,))
    p8(0, x1)
    p7_preload((1,))
    p7(1, x1TH1, side=ln2_steps(x1, 0, True))
    act(small[:, 126:127], small[:, 124:125], AF.Sqrt)
    p8(1, x1H1, partial_stats=True)
    ln2_tail(x1H1, 1)
    return finish()


def _rope_tables():
    half = 32
    inv = (1.0 / (np.float32(10000.0) ** (np.arange(half, dtype=np.float32) / np.float32(half)))).astype(np.float32)
    ang = (np.arange(S_, dtype=np.float32)[:, None] * inv[None, :]).astype(np.float32)
    cos = np.cos(ang).astype(np.float32).T
    sin = np.sin(ang).astype(np.float32).T
    c64 = np.concatenate([cos, cos], 0)
    s64 = np.concatenate([-sin, sin], 0)
    return (np.ascontiguousarray(np.concatenate([c64, c64], 0)),
            np.ascontiguousarray(np.concatenate([s64, s64], 0)))


def _consts():
    ropeC, ropeS = _rope_tables()
    ident = np.eye(128, dtype=np.float32)
    k = np.arange(128)[:, None]
    q = np.arange(256)[None, :]
    MA = np.where(k <= q, 0.0, BIGNEG).astype(np.float32)
    MB = np.where(k + 128 <= q, 0.0, BIGNEG).astype(np.float32)
    mAB = np.ascontiguousarray(np.concatenate([MA, MB], 1))
    oh = (np.arange(S_)[None, :] // 256 == np.arange(8)[:, None]).astype(np.float32)
    am = np.zeros((8, 8), np.float32)
    for b in range(8):
        for n in range(8):
            am[b, n] = 0.0 if n < b else (1e30 if n == b else -1e30)
    am = np.ascontiguousarray(np.broadcast_to(np.tile(am[:, None, :], (1, 8, 1)).reshape(1, 8, 64), (128, 8, 64)))
    return dict(ropeC=ropeC, ropeS=ropeS, ident=ident, mAB=mAB, oh=np.ascontiguousarray(oh), am=am)


def _ptile(w, nk):
    return np.ascontiguousarray(w.reshape(nk, 128, -1).transpose(1, 0, 2))


def prep_shared(inp):
    f = lambda k: np.asarray(inp[k], dtype=np.float32)
    w_in = f("w_in")[0]
    sh = dict(_consts())
    perm = (np.arange(64) + 32) % 64
    tiles = []
    for base in (0, 512):
        for h in range(8):
            cols = np.concatenate([base + h * 64 + np.arange(64), base + h * 64 + perm])
            tiles.append(_ptile(w_in[:, cols], 8))
    sh["wqk"] = np.ascontiguousarray(np.stack(tiles, 0))
    sh["wv"] = _ptile(w_in[:, 1024:1536], 8)
    sh["wu"] = np.ascontiguousarray(np.stack([_ptile(w_in[:, 1536 + g * 128:1536 + (g + 1) * 128], 8) for g in range(4)], 0))
    sh["wg"] = np.ascontiguousarray(np.stack([_ptile(w_in[:, 2048 + t * 128:2048 + (t + 1) * 128], 8) for t in range(16)], 0))
    sh["wba"] = _ptile(f("w_branch_attn")[0], 4)
    sh["wbp"] = _ptile(f("w_branch_pool")[0], 4)
    sh["wpool"] = np.ascontiguousarray(f("w_pool")[0].transpose(1, 0, 2))
    sh["wout"] = _ptile(f("w_out")[0], 8)
    wg_ = f("w_ffn_gate")[0].reshape(8, 128, NF, 128).transpose(2, 1, 0, 3)
    wu_ = f("w_ffn_up")[0].reshape(8, 128, NF, 128).transpose(2, 1, 0, 3)
    sh["wgu"] = np.ascontiguousarray(np.stack([wg_, wu_], 2))
    sh["wd"] = np.ascontiguousarray(f("w_ffn_down")[0].reshape(NF, 128, 8, 128).transpose(2, 1, 0, 3))
    sh["lnp"] = np.ascontiguousarray(np.stack(
        [np.broadcast_to(f(k)[0][None, :], (128, 1024)) for k in ("ln1_g", "ln1_b", "ln2_g", "ln2_b")], 0))
    small = np.zeros((128, 128), np.float32)
    small[:, 0:16] = np.broadcast_to((1.0 / np.arange(1, 17, dtype=np.float32))[None, :], (128, 16))
    small[:, 16:32] = f("b_gate")[0].reshape(16, 128).T
    small[:, 32:36] = f("pool_scale")[0].reshape(4, 128).T
    small[:, 36:58] = f("conv_b")[0].reshape(NF, 128).T
    small[:, 58:124] = f("conv_w")[0].reshape(3, NF, 128).transpose(2, 1, 0).reshape(128, 66)
    sh["small"] = small
    return sh


_CACHE = {}


def _get_nc():
    if "nc" not in _CACHE:
        _CACHE["nc"] = build_program()[0]
    return _CACHE["nc"]


def kernel(**inputs):
    x = np.asarray(inputs["x"], dtype=np.float32)
    sh = prep_shared(inputs)
    in_maps = []
    for b in range(8):
        m = dict(sh)
        m["x"] = np.ascontiguousarray(x[b])
        m["xT"] = np.ascontiguousarray(x[b].T)
        in_maps.append(m)
    nc = _get_nc()
    res = run_bass_kernel_spmd(nc, in_maps, core_ids=list(range(8)))
    return np.stack([np.asarray(r["out"], dtype=np.float32) for r in res.results], 0)
```
